# Optimizing a Trainium2 kernel written in Bass

```python
import math
import jax, jax.numpy as jnp
from jax import lax
import numpy as np

D_MODEL = 1024
BATCH = 2
SEQ = 8192
DEPTH = 1

N_Q_HEADS = 16
N_KV_HEADS = 4
HEAD_DIM = 64
WINDOW = 128
ATTN_BLOCK = 128
ATTN_Q = N_Q_HEADS * HEAD_DIM
ATTN_KV = N_KV_HEADS * HEAD_DIM
SSM_EXPAND = 2
D_INNER = SSM_EXPAND * D_MODEL
SSM_HEAD_DIM = 64
N_SSM_HEADS = D_INNER // SSM_HEAD_DIM
N_SSM_GROUPS = 4
HEADS_PER_GROUP = N_SSM_HEADS // N_SSM_GROUPS
D_STATE = 128
CONV_WIDTH = 4
CHUNK = 128
CONV_DIM = D_INNER + 2 * N_SSM_GROUPS * D_STATE
OFF_Q = 0
OFF_K = OFF_Q + ATTN_Q
OFF_V = OFF_K + ATTN_KV
OFF_Z = OFF_V + ATTN_KV
OFF_XBC = OFF_Z + D_INNER
OFF_DT = OFF_XBC + CONV_DIM
OFF_GA = OFF_DT + N_SSM_HEADS
OFF_GS = OFF_GA + D_MODEL
D_IN_PROJ = OFF_GS + D_MODEL
D_FF = 2816
FFN_RES_WEIGHT = 0.5
LN_EPS = 1e-5
RMS_EPS = 1e-5
DEEPNORM_ALPHA = (2.0 * DEPTH) ** 0.25
DEEPNORM_BETA = (8.0 * DEPTH) ** -0.25

kernel_name = "hybrid_swa_sink_ssd_macaron_deepnorm"


def layer_norm(x, g, b):
    xf = x.astype(jnp.float32)
    mu = jnp.mean(xf, axis=-1, keepdims=True)
    var = jnp.mean(jnp.square(xf - mu), axis=-1, keepdims=True)
    y = (xf - mu) * lax.rsqrt(var + LN_EPS)
    return (y * g.astype(jnp.float32) + b.astype(jnp.float32)).astype(x.dtype)


def swiglu(x, w_gate, w_up, w_down):
    return (jax.nn.silu(x @ w_gate) * (x @ w_up)) @ w_down


def causal_depthwise_conv(u, w, b):
    y = lax.conv_general_dilated(u, w[:, None, :], window_strides=(1,), padding=[(CONV_WIDTH - 1, 0)],
                                 dimension_numbers=("NWC", "WIO", "NWC"), feature_group_count=u.shape[-1])
    return y + b


def sliding_window_sink_attention(q, k, v, sinks):
    b, s = q.shape[0], q.shape[1]
    nb = s // ATTN_BLOCK
    rep = N_Q_HEADS // N_KV_HEADS
    qb = q.reshape(b, nb, ATTN_BLOCK, N_KV_HEADS, rep, HEAD_DIM)
    kb = k.reshape(b, nb, ATTN_BLOCK, N_KV_HEADS, HEAD_DIM)
    vb = v.reshape(b, nb, ATTN_BLOCK, N_KV_HEADS, HEAD_DIM)

    def with_prev(t):
        prev = jnp.pad(t, ((0, 0), (1, 0), (0, 0), (0, 0), (0, 0)))[:, :-1]
        return jnp.concatenate([prev, t], axis=2)

    kk, vv = with_prev(kb), with_prev(vb)
    scores = jnp.einsum("bnqgrd,bnkgd->bngrqk", qb, kk).astype(jnp.float32) * (HEAD_DIM ** -0.5)
    qi = jnp.arange(ATTN_BLOCK)[:, None] + ATTN_BLOCK
    kj = jnp.arange(2 * ATTN_BLOCK)[None, :]
    diff = qi - kj
    band = (diff >= 0) & (diff < WINDOW)
    in_cur = jnp.arange(2 * ATTN_BLOCK) >= ATTN_BLOCK
    mask = band[None] & ((jnp.arange(nb)[:, None, None] > 0) | in_cur[None, None, :])
    scores = jnp.where(mask[None, :, None, None], scores, -jnp.inf)
    sink = sinks.astype(jnp.float32).reshape(N_KV_HEADS, rep)[None, None, :, :, None, None]
    sink = jnp.broadcast_to(sink, scores.shape[:-1] + (1,))
    probs = jax.nn.softmax(jnp.concatenate([scores, sink], axis=-1), axis=-1)[..., :-1]
    out = jnp.einsum("bngrqk,bnkgd->bnqgrd", probs.astype(v.dtype), vv)
    return out.reshape(b, s, ATTN_Q)


def segsum_exp(a):
    t = a.shape[-1]
    rep = jnp.broadcast_to(a[..., None], a.shape + (t,))
    strict = jnp.tril(jnp.ones((t, t), dtype=bool), -1)
    cs = jnp.cumsum(jnp.where(strict, rep, 0.0), axis=-2)
    incl = jnp.tril(jnp.ones((t, t), dtype=bool), 0)
    return jnp.exp(jnp.where(incl, cs, -jnp.inf))


def ssd_chunked(x, dt, A, B, C):
    b, L = x.shape[0], x.shape[1]
    nc = L // CHUNK
    G, K, P, N = N_SSM_GROUPS, HEADS_PER_GROUP, SSM_HEAD_DIM, D_STATE
    xdt = (x * dt[..., None]).reshape(b, nc, CHUNK, G, K, P)
    dA = (dt * A).reshape(b, nc, CHUNK, G, K).transpose(0, 1, 3, 4, 2)
    Bc = B.reshape(b, nc, CHUNK, G, N)
    Cc = C.reshape(b, nc, CHUNK, G, N)
    A_cum = jnp.cumsum(dA, axis=-1)
    Lmat = segsum_exp(dA)
    CB = jnp.einsum("bclgn,bcsgn->bcgls", Cc, Bc)
    y_diag = jnp.einsum("bcgls,bcgkls,bcsgkp->bclgkp", CB, Lmat, xdt)
    decay_states = jnp.exp(A_cum[..., -1:] - A_cum)
    states = jnp.einsum("bclgn,bcgkl,bclgkp->bcgkpn", Bc, decay_states, xdt)
    chunk_decay = jnp.exp(A_cum[..., -1])

    def step(h, inp):
        s_c, d_c = inp
        return d_c[..., None, None] * h + s_c, h

    h0 = jnp.zeros((b, G, K, P, N), dtype=x.dtype)
    _, prev = lax.scan(step, h0, (jnp.moveaxis(states, 1, 0), jnp.moveaxis(chunk_decay, 1, 0)))
    prev = jnp.moveaxis(prev, 0, 1)
    y_off = jnp.einsum("bclgn,bcgkpn,bcgkl->bclgkp", Cc, prev, jnp.exp(A_cum))
    return (y_diag + y_off).reshape(b, L, G, K, P)


def ssd_branch(z, xbc, dt_raw, conv_w, conv_b, dt_bias, a_log, d_skip, norm_w):
    b, L = z.shape[0], z.shape[1]
    xbc = jax.nn.silu(causal_depthwise_conv(xbc, conv_w, conv_b))
    xs = xbc[..., :D_INNER]
    Bs = xbc[..., D_INNER:D_INNER + N_SSM_GROUPS * D_STATE].reshape(b, L, N_SSM_GROUPS, D_STATE)
    Cs = xbc[..., D_INNER + N_SSM_GROUPS * D_STATE:].reshape(b, L, N_SSM_GROUPS, D_STATE)
    xh = xs.astype(jnp.float32).reshape(b, L, N_SSM_GROUPS, HEADS_PER_GROUP, SSM_HEAD_DIM)
    dt = jax.nn.softplus(dt_raw.astype(jnp.float32) + dt_bias.astype(jnp.float32))
    dt = dt.reshape(b, L, N_SSM_GROUPS, HEADS_PER_GROUP)
    A = -jnp.exp(a_log.astype(jnp.float32)).reshape(N_SSM_GROUPS, HEADS_PER_GROUP)
    y = ssd_chunked(xh, dt, A, Bs.astype(jnp.float32), Cs.astype(jnp.float32))
    y = y + d_skip.astype(jnp.float32).reshape(N_SSM_GROUPS, HEADS_PER_GROUP)[..., None] * xh
    u = (y.reshape(b, L, D_INNER) * jax.nn.silu(z.astype(jnp.float32))).reshape(b, L, N_SSM_GROUPS, -1)
    u = u * lax.rsqrt(jnp.mean(jnp.square(u), axis=-1, keepdims=True) + RMS_EPS)
    u = u.reshape(b, L, D_INNER) * norm_w.astype(jnp.float32)
    return u.astype(z.dtype)


def hybrid_mixer(h, w_in, conv_w, conv_b, dt_bias, a_log, d_skip, ssm_norm_w, sinks, w_attn_branch, w_ssm_branch, w_out):
    b, s = h.shape[0], h.shape[1]
    proj = h @ w_in
    q = proj[..., OFF_Q:OFF_K].reshape(b, s, N_Q_HEADS, HEAD_DIM)
    k = proj[..., OFF_K:OFF_V].reshape(b, s, N_KV_HEADS, HEAD_DIM)
    v = proj[..., OFF_V:OFF_Z].reshape(b, s, N_KV_HEADS, HEAD_DIM)
    z = proj[..., OFF_Z:OFF_XBC]
    xbc = proj[..., OFF_XBC:OFF_DT]
    dt_raw = proj[..., OFF_DT:OFF_GA]
    gate_attn = jax.nn.sigmoid(proj[..., OFF_GA:OFF_GS])
    gate_ssm = jax.nn.sigmoid(proj[..., OFF_GS:])
    y_attn = sliding_window_sink_attention(q, k, v, sinks) @ w_attn_branch
    y_ssm = ssd_branch(z, xbc, dt_raw, conv_w, conv_b, dt_bias, a_log, d_skip, ssm_norm_w) @ w_ssm_branch
    merged = gate_attn * y_attn + gate_ssm * y_ssm
    return merged @ w_out


def _normal(k, shape, scale):
    return jax.random.normal(k, shape, jnp.float32) * scale


def setup_inputs(seed: int = 0) -> dict:
    key = jax.random.key(seed)
    ks = jax.random.split(key, 32)
    ffn_in = D_MODEL ** -0.5
    ffn_out = (D_FF ** -0.5) * DEEPNORM_BETA
    u = jax.random.uniform(ks[10], (DEPTH, N_SSM_HEADS), jnp.float32)
    dt0 = jnp.exp(u * (math.log(0.1) - math.log(0.001)) + math.log(0.001))
    dt_bias = dt0 + jnp.log(-jnp.expm1(-dt0))
    return {
        "x": jax.random.normal(ks[0], (BATCH, SEQ, D_MODEL), jnp.float32),
        "ffn1_w_gate": _normal(ks[1], (DEPTH, D_MODEL, D_FF), ffn_in),
        "ffn1_w_up": _normal(ks[2], (DEPTH, D_MODEL, D_FF), ffn_in),
        "ffn1_w_down": _normal(ks[3], (DEPTH, D_FF, D_MODEL), ffn_out),
        "ln1_g": 1.0 + _normal(ks[4], (DEPTH, D_MODEL), 0.02),
        "ln1_b": _normal(ks[5], (DEPTH, D_MODEL), 0.02),
        "w_in": _normal(ks[6], (DEPTH, D_MODEL, D_IN_PROJ), D_MODEL ** -0.5),
        "conv_w": _normal(ks[7], (DEPTH, CONV_WIDTH, CONV_DIM), CONV_WIDTH ** -0.5),
        "conv_b": _normal(ks[8], (DEPTH, CONV_DIM), 0.02),
        "dt_bias": dt_bias,
        "a_log": jnp.log(jax.random.uniform(ks[11], (DEPTH, N_SSM_HEADS), jnp.float32, 1.0, 16.0)),
        "d_skip": 1.0 + _normal(ks[12], (DEPTH, N_SSM_HEADS), 0.02),
        "ssm_norm_w": 1.0 + _normal(ks[13], (DEPTH, D_INNER), 0.02),
        "sinks": _normal(ks[14], (DEPTH, N_Q_HEADS), 0.5),
        "w_attn_branch": _normal(ks[15], (DEPTH, ATTN_Q, D_MODEL), ATTN_Q ** -0.5),
        "w_ssm_branch": _normal(ks[16], (DEPTH, D_INNER, D_MODEL), D_INNER ** -0.5),
        "w_out": _normal(ks[17], (DEPTH, D_MODEL, D_MODEL), (D_MODEL ** -0.5) * DEEPNORM_BETA),
        "ln2_g": 1.0 + _normal(ks[18], (DEPTH, D_MODEL), 0.02),
        "ln2_b": _normal(ks[19], (DEPTH, D_MODEL), 0.02),
        "ffn2_w_gate": _normal(ks[20], (DEPTH, D_MODEL, D_FF), ffn_in),
        "ffn2_w_up": _normal(ks[21], (DEPTH, D_MODEL, D_FF), ffn_in),
        "ffn2_w_down": _normal(ks[22], (DEPTH, D_FF, D_MODEL), ffn_out),
        "ln3_g": 1.0 + _normal(ks[23], (DEPTH, D_MODEL), 0.02),
        "ln3_b": _normal(ks[24], (DEPTH, D_MODEL), 0.02),
    }


def reference(x, ffn1_w_gate, ffn1_w_up, ffn1_w_down, ln1_g, ln1_b, w_in, conv_w, conv_b, dt_bias, a_log, d_skip,
              ssm_norm_w, sinks, w_attn_branch, w_ssm_branch, w_out, ln2_g, ln2_b, ffn2_w_gate, ffn2_w_up,
              ffn2_w_down, ln3_g, ln3_b):
    for i in range(DEPTH):
        x = layer_norm(DEEPNORM_ALPHA * x + FFN_RES_WEIGHT * swiglu(x, ffn1_w_gate[i], ffn1_w_up[i], ffn1_w_down[i]),
                       ln1_g[i], ln1_b[i])
        mix = hybrid_mixer(x, w_in[i], conv_w[i], conv_b[i], dt_bias[i], a_log[i], d_skip[i], ssm_norm_w[i],
                           sinks[i], w_attn_branch[i], w_ssm_branch[i], w_out[i])
        x = layer_norm(DEEPNORM_ALPHA * x + mix, ln2_g[i], ln2_b[i])
        x = layer_norm(DEEPNORM_ALPHA * x + FFN_RES_WEIGHT * swiglu(x, ffn2_w_gate[i], ffn2_w_up[i], ffn2_w_down[i]),
                       ln3_g[i], ln3_b[i])
    return x
```

```python
import numpy as np
from contextlib import ExitStack
import concourse.bass as bass
import concourse.mybir as mybir
from concourse.bass_utils import run_bass_kernel_spmd

D = 1024; DFF = 2816; NCORES = 8; SEQ = 8192; BATCH = 2
OWN = 2048; CH = 128; N_OWN_CH = OWN // CH
HALO = 128
PREFIX_MAX = SEQ - OWN; N_PRE_CH = PREFIX_MAX // CH
ALPHA = 2.0 ** 0.25; LN_EPS = 1e-5; RMS_EPS = 1e-5; HEAD_SCALE = 64 ** -0.5
LABEL_PREFIX = "no x-core exchange avail -> redundant SSD prefix recompute (~2.1x FLOPs/core); "


F32 = mybir.dt.float32
BF16 = mybir.dt.bfloat16
AF = mybir.ActivationFunctionType
ALU = mybir.AluOpType
AX = mybir.AxisListType


class Tracker:
    def __init__(self, nc, stack, n_dma_sems=40):
        self.nc = nc
        self.eng = {"pe": nc.tensor, "act": nc.scalar, "dve": nc.vector, "pool": nc.gpsimd, "sp": nc.sync}
        self.sem = {e: stack.enter_context(nc.semaphore(f"s_{e}")) for e in self.eng}
        self.cnt = {e: 0 for e in self.eng}
        self.dma_sems = [stack.enter_context(nc.semaphore(f"s_dma{i}")) for i in range(n_dma_sems)]
        self.dma_cnt = [0] * n_dma_sems
        self.dma_rr = 0
        self.known = {e: {} for e in self.eng}
        self.last_w = {}
        self.readers = {}
        self.n_waits = 0
        self.n_ops = 0

    def _wait(self, e, tok):
        sem, val, _src = tok
        k = self.known[e]
        if k.get(sem.name, 0) >= val:
            return
        self.eng[e].wait_ge(sem, val)
        k[sem.name] = val
        self.n_waits += 1

    def _deps(self, e, reads, writes, is_dma):
        raw, other = [], []
        for r in reads:
            w = self.last_w.get(r)
            if w is not None:
                raw.append(w)
        for w_ in writes:
            w = self.last_w.get(w_)
            if w is not None:
                other.append(w)
            other.extend(self.readers.get(w_, ()))
        for tok in raw:
            if tok[2] == e and e == "pe" and not is_dma:
                continue
            self._wait(e, tok)
        for tok in other:
            if tok[2] == e and not is_dma:
                continue
            self._wait(e, tok)

    def _register(self, tok, reads, writes):
        for r in reads:
            self.readers.setdefault(r, []).append(tok)
        for w_ in writes:
            self.last_w[w_] = tok
            self.readers[w_] = []

    def op(self, e, fn, reads=(), writes=()):
        self._deps(e, reads, writes, False)
        ins = fn(self.eng[e])
        self.cnt[e] += 1
        ins.then_inc(self.sem[e], 1)
        tok = (self.sem[e], self.cnt[e], e)
        self._register(tok, reads, writes)
        self.n_ops += 1
        return tok

    def multi(self, e, fns, reads=(), writes=()):
        self._deps(e, reads, writes, False)
        ins = None
        for fn in fns:
            ins = fn(self.eng[e])
        self.cnt[e] += 1
        ins.then_inc(self.sem[e], 1)
        tok = (self.sem[e], self.cnt[e], e)
        self._register(tok, reads, writes)
        self.n_ops += 1
        return tok

    def dma(self, q, out, in_, reads=(), writes=(), **kw):
        self._deps(q, reads, writes, True)
        i = self.dma_rr
        self.dma_rr = (self.dma_rr + 1) % len(self.dma_sems)
        sem = self.dma_sems[i]
        if self.dma_cnt[i] > 0:
            self._wait(q, (sem, 16 * self.dma_cnt[i], "dma"))
        self.dma_cnt[i] += 1
        ins = self.eng[q].dma_start(out=out, in_=in_, **kw)
        ins.then_inc(sem, 16)
        tok = (sem, 16 * self.dma_cnt[i], "dma")
        self._register(tok, reads, writes)
        self.n_ops += 1
        return tok

    def barrier(self):
        toks = [(self.sem[e], self.cnt[e], e) for e in self.eng if self.cnt[e] > 0]
        toks += [(s, 16 * c, "dma") for s, c in zip(self.dma_sems, self.dma_cnt) if c > 0]
        for e in self.eng:
            for tok in toks:
                if tok[2] == e:
                    continue
                self._wait(e, tok)
        self.last_w.clear()
        self.readers.clear()

    def final_wait(self, e="sp"):
        toks = [(self.sem[x], self.cnt[x], x) for x in self.eng if self.cnt[x] > 0 and x != e]
        toks += [(s, 16 * c, "dma") for s, c in zip(self.dma_sems, self.dma_cnt) if c > 0]
        for tok in toks:
            self._wait(e, tok)


def _core_layout(x):
    per_core = []
    for c in range(NCORES):
        b, p = divmod(c, SEQ // OWN)
        t0 = p * OWN
        own = np.ascontiguousarray(x[b, t0:t0 + OWN])
        halo = np.zeros((HALO, D), np.float32)
        if p > 0:
            halo[:] = x[b, t0 - HALO:t0]
        prefix = np.zeros((PREFIX_MAX, D), np.float32)
        if p > 0:
            prefix[:t0] = x[b, :t0]
        pre_valid = np.zeros((1, N_PRE_CH), np.float32)
        pre_valid[0, :t0 // CH] = 1.0
        has_prev = np.full((1, 1), 1.0 if p > 0 else 0.0, np.float32)
        per_core.append({"x_own": own, "x_halo": halo, "x_prefix": prefix,
                         "pre_valid": pre_valid, "has_prev": has_prev})
    return per_core


def _const_inputs():
    i = np.arange(CH)
    return {
        "c_ident": np.eye(CH, dtype=np.float32),
        "c_tri_le": (i[:, None] <= i[None, :]).astype(np.float32),
        "c_tri_gt": (i[:, None] > i[None, :]).astype(np.float32),
    }


OFF_Q = 0; OFF_K = 1024; OFF_V = 1280; OFF_Z = 1536; OFF_XBC = 3584; OFF_DT = 6656; OFF_GA = 6688; OFF_GS = 7712
NFF = DFF // 128


def _split_blocks(n, mx=4):
    nb = -(-n // mx)
    base, rem = divmod(n, nb)
    sizes = [base + (1 if i < rem else 0) for i in range(nb)]
    out, s = [], 0
    for z in sizes:
        out.append((s, z)); s += z
    return out


def build(own_ch=N_OWN_CH, pre_ch=N_PRE_CH, upto="all", dbg=()):
    nc = bass.Bass("TRN2", target_bir_lowering=False)
    NT_OWN = own_ch * CH
    NT_PRE = pre_ch * CH

    def din(name, shape):
        return nc.dram_tensor(name, list(shape), F32, kind="ExternalInput").ap()

    x_own = din("x_own", [NT_OWN, D]); x_halo = din("x_halo", [HALO, D]); x_prefix = din("x_prefix", [NT_PRE, D])
    pre_valid = din("pre_valid", [1, pre_ch]); has_prev = din("has_prev", [1, 1])
    c_ident = din("c_ident", [CH, CH]); c_tri_le = din("c_tri_le", [CH, CH]); c_tri_gt = din("c_tri_gt", [CH, CH])
    f1g = din("ffn1_wg", [D, DFF]); f1u = din("ffn1_wu", [D, DFF]); f1d = din("ffn1_wd", [DFF, D])
    f2g = din("ffn2_wg", [D, DFF]); f2u = din("ffn2_wu", [D, DFF]); f2d = din("ffn2_wd", [DFF, D])
    w_in = din("w_in", [D, 8736]); w_ab = din("w_ab", [D, D]); w_sb = din("w_sb", [2048, D]); w_o = din("w_o", [D, D])
    ln1_gc = din("ln1_gc", [128, 8]); ln1_bc = din("ln1_bc", [128, 8]); ln2_gc = din("ln2_gc", [128, 8]); ln2_bc = din("ln2_bc", [128, 8])
    ln3_gr = din("ln3_gr", [1, D]); ln3_br = din("ln3_br", [1, D])
    conv_wc = din("conv_wc", [128, 96]); conv_bc = din("conv_bc", [128, 24])
    dtb_r = din("dtb_r", [1, 32]); alog_r = din("alog_r", [1, 32]); dsk_r = din("dsk_r", [1, 32]); sinks_r = din("sinks_r", [1, 16])
    normw_c = din("normw_c", [128, 16])
    out = nc.dram_tensor("out", [NT_OWN, D], F32, kind="ExternalOutput").ap()
    dbg_outs = {}

    win_v = w_in.rearrange("(kc p) n -> p kc n", p=128)

    with ExitStack() as st:
        T = Tracker(nc, st, n_dma_sems=48)

        def sbt(stack, name, shape, dt=F32):
            return stack.enter_context(nc.sbuf_tensor(name, list(shape), dt))

        PS = [st.enter_context(nc.psum_tensor(f"PS{i}", [128, 512], F32)) for i in range(8)]
        PSb = [p[:].bitcast(BF16) for p in PS]
        st.enter_context(nc.Block())
        rrc = {}

        def rr(key, n):
            v = rrc.get(key, 0); rrc[key] = v + 1
            return v % n

        def mm(o, lhsT, rhs, start=True, stop=True):
            return lambda e: e.matmul(o, lhsT=lhsT, rhs=rhs, start=start, stop=stop)

        def trp(o, i, ident):
            return lambda e: e.transpose(out=o, in_=i, identity=ident)

        def act(o, i, func, **kw):
            return lambda e: e.activation(out=o, in_=i, func=func, **kw)

        def tt(o, a, b, op):
            return lambda e: e.tensor_tensor(out=o, in0=a, in1=b, op=op)

        def ts(o, a, s1, s2, op0, op1=None):
            if op1 is None:
                return lambda e: e.tensor_scalar(out=o, in0=a, scalar1=s1, scalar2=None, op0=op0)
            return lambda e: e.tensor_scalar(out=o, in0=a, scalar1=s1, scalar2=s2, op0=op0, op1=op1)

        def stt(o, a, s, b, op0, op1):
            return lambda e: e.scalar_tensor_tensor(out=o, in0=a, scalar=s, in1=b, op0=op0, op1=op1)

        def cp(o, i):
            return lambda e: e.tensor_copy(out=o, in_=i)

        def dump(name, ap, shape, reads):
            d = nc.dram_tensor("dbg_" + name, list(shape), F32, kind="ExternalOutput").ap()
            tmp = sbt(st, "dbgt_" + name, shape, F32)
            T.op("act", act(tmp[:], ap, AF.Copy), reads=reads, writes=["dbgt_" + name])
            T.dma("sp", d, tmp[:], reads=["dbgt_" + name], writes=["dbgd_" + name])
            dbg_outs[name] = d

        idf = sbt(st, "idf", [128, 128]); idb = sbt(st, "idb", [128, 128], BF16)
        trile = sbt(st, "trile", [128, 128]); trigt = sbt(st, "trigt", [128, 128]); ones = sbt(st, "ones", [128, 128])
        mcur = sbt(st, "mcur", [128, 128], BF16); mprev = sbt(st, "mprev", [128, 128], BF16); mprevf = sbt(st, "mprevf", [128, 128], BF16)
        hp_bc = sbt(st, "hp_bc", [128, 1]); valid_bc = sbt(st, "valid_bc", [128, pre_ch])
        dtb_bc = sbt(st, "dtb_bc", [128, 32]); A_bc = sbt(st, "A_bc", [128, 32]); dsk_bc = sbt(st, "dsk_bc", [128, 32])
        esink_bc = sbt(st, "esink_bc", [128, 16])
        cw = sbt(st, "cw", [128, 24, 4]); cb = sbt(st, "cb", [128, 24]); normw = sbt(st, "normw", [128, 16])
        g1c = sbt(st, "g1c", [128, 8]); b1c = sbt(st, "b1c", [128, 8]); g2c = sbt(st, "g2c", [128, 8]); b2c = sbt(st, "b2c", [128, 8])
        Wdt = sbt(st, "Wdt", [128, 8, 32], BF16)
        H = sbt(st, "Hst", [128, 4, 512]); Hbf = sbt(st, "Hbf", [128, 4, 512], BF16); Hbf2 = sbt(st, "Hbf2", [128, 512], BF16)

        T.dma("sp", idf[:], c_ident[:, :], writes=["idf"])
        T.dma("sp", trile[:], c_tri_le[:, :], writes=["trile"])
        T.dma("sp", trigt[:], c_tri_gt[:, :], writes=["trigt"])
        T.dma("sp", hp_bc[:], has_prev[0, :].partition_broadcast(128), writes=["hp"])
        T.dma("sp", valid_bc[:], pre_valid[0, :].partition_broadcast(128), writes=["valid"])
        T.dma("sp", dtb_bc[:], dtb_r[0, :].partition_broadcast(128), writes=["dtb"])
        T.dma("sp", A_bc[:], alog_r[0, :].partition_broadcast(128), writes=["A0"])
        T.dma("sp", dsk_bc[:], dsk_r[0, :].partition_broadcast(128), writes=["dsk"])
        T.dma("sp", esink_bc[:], sinks_r[0, :].partition_broadcast(128), writes=["esink0"])
        T.dma("sp", cw[:].rearrange("p c j -> p (c j)"), conv_wc[:, :], writes=["cw"])
        T.dma("sp", cb[:], conv_bc[:, :], writes=["cb"])
        T.dma("sp", normw[:], normw_c[:, :], writes=["normw"])
        for (t_, d_, n_) in ((g1c, ln1_gc, "g1c"), (b1c, ln1_bc, "b1c"), (g2c, ln2_gc, "g2c"), (b2c, ln2_bc, "b2c")):
            T.dma("sp", t_[:], d_[:, :], writes=[n_])
        T.dma("pool", Wdt[:], win_v[:, :, OFF_DT:OFF_DT + 32], writes=["Wdt"])
        T.op("dve", cp(idb[:], idf[:]), reads=["idf"], writes=["idb"])
        T.op("dve", lambda e: e.memset(ones[:], 1.0), writes=["ones"])
        T.op("dve", cp(mcur[:], trile[:]), reads=["trile"], writes=["mcur"])
        T.op("dve", ts(mprev[:], trigt[:], hp_bc[:, 0:1], None, ALU.mult), reads=["trigt", "hp"], writes=["mprev"])
        T.op("dve", cp(mprevf[:], trigt[:]), reads=["trigt"], writes=["mprevf"])
        T.op("act", act(A_bc[:], A_bc[:], AF.Exp), reads=["A0"], writes=["A1"])
        T.op("dve", ts(A_bc[:], A_bc[:], -1.0, None, ALU.mult), reads=["A1"], writes=["A"])
        T.op("act", act(esink_bc[:], esink_bc[:], AF.Exp), reads=["esink0"], writes=["esink"])
        T.op("dve", lambda e: e.memset(H[:], 0.0), writes=["H0", "H1", "H2", "H3"])

        if upto == "setup":
            dump("A", A_bc[:], [128, 32], ["A"])
            T.final_wait("sp")
            return nc, dbg_outs

        def ffn_alloc(ph, tag, nring=3):
            B = {"nring": nring}
            B["wg"] = [sbt(ph, f"{tag}wg{i}", [128, 8, 256], BF16) for i in range(nring)]
            B["wu"] = [sbt(ph, f"{tag}wu{i}", [128, 8, 256], BF16) for i in range(nring)]
            B["wd"] = sbt(ph, f"{tag}wd", [128, NFF, 1024], BF16)
            B["actT"] = sbt(ph, f"{tag}actT", [128, NFF, 512], BF16)
            B["sg"] = [sbt(ph, f"{tag}sg{i}", [128, 512]) for i in range(2)]
            B["pre"] = [sbt(ph, f"{tag}pre{i}", [128, 1024]) for i in range(2)]
            B["st6"] = sbt(ph, f"{tag}st6", [128, 2, 6]); B["mv"] = sbt(ph, f"{tag}mv", [128, 4])
            return B

        def load_wd(B, wd_dram):
            v = wd_dram.rearrange("(fc p) n -> p fc n", p=128)
            for f0 in range(0, NFF, 8):
                f1 = min(NFF, f0 + 8)
                T.dma("pool", B["wd"][:, f0:f1, :], v[:, f0:f1, :], writes=["wd"])

        def ln_tail(B, eps_eff, mode, pre, pname, part="all", **kw):
            st6, mv = B["st6"], B["mv"]
            pre_reads = [pname + "a", pname + "b"]
            if part in ("all", "stats"):
                ln_stats(B, eps_eff, pre, pre_reads)
            if part == "stats":
                return
            ln_rest(B, mode, pre, pre_reads, **kw)

        def ln_stats(B, eps_eff, pre, pre_reads):
            st6, mv = B["st6"], B["mv"]
            T.op("dve", lambda e: e.bn_stats(out=st6[:, 0, :], in_=pre[:, 0:512]), reads=pre_reads, writes=["st6a"])
            T.op("dve", lambda e: e.bn_stats(out=st6[:, 1, :], in_=pre[:, 512:1024]), reads=pre_reads, writes=["st6b"])
            T.op("dve", lambda e: e.bn_aggr(out=mv[:, 0:2], in_=st6[:].rearrange("p a b -> p (a b)")), reads=["st6a", "st6b"], writes=["mv01"])
            T.op("act", act(mv[:, 2:3], mv[:, 1:2], AF.Sqrt, bias=float(eps_eff)), reads=["mv01"], writes=["mv2"])
            T.op("dve", lambda e: e.reciprocal(out=mv[:, 3:4], in_=mv[:, 2:3]), reads=["mv2"], writes=["mv3"])
            T.op("dve", ts(pre[:], pre[:], mv[:, 0:1], mv[:, 3:4], ALU.subtract, ALU.mult), reads=pre_reads + ["mv01", "mv3"], writes=pre_reads)

        def ln_rest(B, mode, pre, pre_reads, **kw):
            if mode == "hT":
                dst, dcol, gcol, bcol, gname, dres = kw["dst"], kw["dcol"], kw["gcol"], kw["bcol"], kw["gname"], kw["dres"]
                for half in range(2):
                    pb = PS[2 + half]
                    T.multi("pe", [trp(pb[:, j * 128:(j + 1) * 128], pre[:, (half * 4 + j) * 128:(half * 4 + j + 1) * 128], idf[:]) for j in range(4)],
                            reads=pre_reads + ["idf"], writes=[f"PS{2 + half}"])
                    for j in range(4):
                        kc = half * 4 + j
                        T.op("dve", ts(dst[:, kc, dcol:dcol + 128], pb[:, j * 128:(j + 1) * 128], gcol[:, kc:kc + 1], bcol[:, kc:kc + 1], ALU.mult, ALU.add),
                             reads=[f"PS{2 + half}"] + gname, writes=[dres])
            else:
                g_bc, b_bc, orow, yo = kw["g_bc"], kw["b_bc"], kw["orow"], kw["yo"]
                T.op("dve", tt(pre[:], pre[:], g_bc[:], ALU.mult), reads=pre_reads + ["g3"], writes=pre_reads)
                T.op("dve", tt(yo[:], pre[:], b_bc[:], ALU.add), reads=pre_reads + ["b3"], writes=[kw["yoname"]])
                T.dma("sp", out[orow:orow + 128, :], yo[:], reads=[kw["yoname"]], writes=["out_rows"])

        def ffn_block(B, wg_d, wu_d, nch, xT_fn, xT_reads, resid, eps_eff, tail_fn, bg=None, bg_per=2, after_first=None):
            N = nch * 128
            wgv = wg_d.rearrange("(kc p) n -> p kc n", p=128); wuv = wu_d.rearrange("(kc p) n -> p kc n", p=128)
            for f0 in range(0, NFF, 2):
                slot = rr("wring" + str(B["nring"]), B["nring"])
                T.dma("pool", B["wg"][slot][:], wgv[:, :, f0 * 128:(f0 + 2) * 128], writes=[f"wg{slot}"])
                T.dma("pool", B["wu"][slot][:], wuv[:, :, f0 * 128:(f0 + 2) * 128], writes=[f"wu{slot}"])
                for f in range(2):
                    ffc = f0 + f
                    pi = rr("gu", 2)
                    pg, pu = (PS[0], PS[1]) if pi == 0 else (PS[4], PS[5])
                    pgn, pun = ("PS0", "PS1") if pi == 0 else ("PS4", "PS5")
                    T.multi("pe", [mm(pg[:, :N], B["wg"][slot][:, kc, f * 128:(f + 1) * 128], xT_fn(kc), kc == 0, kc == 7) for kc in range(8)],
                            reads=[f"wg{slot}"] + xT_reads, writes=[pgn])
                    T.multi("pe", [mm(pu[:, :N], B["wu"][slot][:, kc, f * 128:(f + 1) * 128], xT_fn(kc), kc == 0, kc == 7) for kc in range(8)],
                            reads=[f"wu{slot}"] + xT_reads, writes=[pun])
                    si = rr("sg", 2)
                    T.op("act", act(B["sg"][si][:, :N], pg[:, :N], AF.Silu), reads=[pgn], writes=[f"sg{si}"])
                    T.op("dve", stt(B["actT"][:, ffc, :N], B["sg"][si][:, :N], 0.5 / ALPHA, pu[:, :N], ALU.mult, ALU.mult),
                         reads=[f"sg{si}", pun], writes=[f"actT{ffc}"])
                    if ffc == 0 and after_first is not None:
                        after_first()
                    if bg is not None:
                        for _ in range(bg_per):
                            next(bg, None)
            if bg is not None:
                for _ in bg:
                    pass
            act_reads = [f"actT{f}" for f in range(NFF)]
            for c in range(nch):
                pre = B["pre"][c % 2]; pname = f"pre{c % 2}"
                if resid[0] == "dram":
                    xr, xrn = resid[1](c)
                for half in range(2):
                    pd = PS[6 + half]; pdn = f"PS{6 + half}"
                    fns = [mm(pd[:, :], B["actT"][:, f, c * 128:(c + 1) * 128], B["wd"][:, f, half * 512:(half + 1) * 512], f == 0, (f == NFF - 1) and resid[0] != "ident")
                           for f in range(NFF)]
                    rds = act_reads + ["wd"]
                    if resid[0] == "ident":
                        src_fn, src_reads = resid[1], resid[2]
                        fns += [mm(pd[:, j * 128:(j + 1) * 128], src_fn(half * 4 + j, c), idb[:], False, j == 3) for j in range(4)]
                        rds = rds + src_reads + ["idb"]
                    T.multi("pe", fns, reads=rds, writes=[pdn])
                if c >= 1:
                    tail_fn(c - 1, B["pre"][(c - 1) % 2], f"pre{(c - 1) % 2}", "stats")
                for half in range(2):
                    pd = PS[6 + half]; pdn = f"PS{6 + half}"
                    hn = pname + ("a" if half == 0 else "b")
                    if resid[0] == "dram":
                        T.op("dve", tt(pre[:, half * 512:(half + 1) * 512], pd[:, :], xr[:, half * 512:(half + 1) * 512], ALU.add),
                             reads=[pdn, xrn, pname + "a", pname + "b"], writes=[hn])
                    else:
                        T.op("act", act(pre[:, half * 512:(half + 1) * 512], pd[:, :], AF.Copy), reads=[pdn, pname + "a", pname + "b"], writes=[hn])
                if c >= 1:
                    tail_fn(c - 1, B["pre"][(c - 1) % 2], f"pre{(c - 1) % 2}", "rest")
            tail_fn(nch - 1, B["pre"][(nch - 1) % 2], f"pre{(nch - 1) % 2}", "stats")
            return lambda: tail_fn(nch - 1, B["pre"][(nch - 1) % 2], f"pre{(nch - 1) % 2}", "rest")

        def ssd_alloc(ph, tag, nfeat):
            S = {}
            S["raw"] = [sbt(ph, f"{tag}raw{i}", [128, 520]) for i in range(2)]
            S["acc"] = [sbt(ph, f"{tag}acc{i}", [128, 512]) for i in range(2)]
            S["feat"] = sbt(ph, f"{tag}feat", [128, nfeat, 512], BF16)
            S["dtx"] = sbt(ph, f"{tag}dtx", [128, 4, 32]); S["dte"] = sbt(ph, f"{tag}dte", [128, 4, 32]); S["dt"] = sbt(ph, f"{tag}dt", [128, 4, 32])
            S["dtv"] = sbt(ph, f"{tag}dtv", [128, 4, 32]); S["dA"] = sbt(ph, f"{tag}dA", [128, 4, 32])
            S["ea"] = sbt(ph, f"{tag}ea", [128, 4, 3, 32]); S["coef"] = sbt(ph, f"{tag}coef", [128, 4, 32])
            S["xdd"] = [sbt(ph, f"{tag}xdd{i}", [128, 512], BF16) for i in range(2)]
            S["btok"] = [sbt(ph, f"{tag}btok{i}", [128, 128], BF16) for i in range(2)]
            S["htmp"] = sbt(ph, f"{tag}htmp", [128, 512])
            return S

        def ssd_front(S, hT_fn, hT_reads, N, ccs, hist, hname, only_hist=False, praw=(0, 1)):
            for (fi, chidx, lhs_fn, wres) in ccs:
                pi = praw[rr("praw" + str(len(praw)), len(praw))]; pr = PS[pi]; prn = f"PS{pi}"
                T.multi("pe", [mm(pr[:, :N], lhs_fn(kc), hT_fn(kc), kc == 0, kc == 7) for kc in range(8)], reads=[wres] + hT_reads, writes=[prn])
                hres = f"{hname}{chidx}"
                if only_hist:
                    T.op("dve", ts(hist[:, chidx, :], pr[:, N - 3:N], hp_bc[:, 0:1], None, ALU.mult), reads=[prn, "hp"], writes=[hres])
                    continue
                ri = rr("raw", 2); raw = S["raw"][ri]
                T.op("act", act(raw[:, 3:3 + N], pr[:, :N], AF.Copy), reads=[prn], writes=[f"raw{ri}b"])
                T.op("dve", cp(raw[:, 0:3], hist[:, chidx, :]), reads=[hres], writes=[f"raw{ri}a"])
                T.op("dve", cp(hist[:, chidx, :], raw[:, N:N + 3]), reads=[f"raw{ri}b"], writes=[hres])
                ai = rr("acc", 2); acc = S["acc"][ai]; an = f"acc{ai}"
                T.op("dve", ts(acc[:, :N], raw[:, 3:3 + N], cw[:, chidx, 3:4], cb[:, chidx:chidx + 1], ALU.mult, ALU.add),
                     reads=[f"raw{ri}b", "cw", "cb"], writes=[an])
                for j in (2, 1, 0):
                    T.op("dve", stt(acc[:, :N], raw[:, j:j + N], cw[:, chidx, j:j + 1], acc[:, :N], ALU.mult, ALU.add),
                         reads=[f"raw{ri}a", f"raw{ri}b", an, "cw"], writes=[an])
                T.op("act", act(S["feat"][:, fi, :N], acc[:, :N], AF.Silu), reads=[an], writes=[f"feat{fi}"])

        def ssd_decay(S, hTc_fn, hT_reads, nch, hs0, nh, vidx0):
            psd = PS[2]
            for c in range(nch):
                T.multi("pe", [mm(psd[:, c * 32:c * 32 + nh], hTc_fn(kc, c), Wdt[:, kc, hs0:hs0 + nh], kc == 0, kc == 7) for kc in range(8)],
                        reads=hT_reads + ["Wdt"], writes=["PS2"])
            pv = psd[:, 0:nch * 32].rearrange("p (c h) -> p c h", h=32)[:, :, 0:nh]
            T.op("dve", tt(S["dtx"][:, :nch, :nh], pv, dtb_bc[:, hs0:hs0 + nh].unsqueeze(1).broadcast_to([128, nch, nh]), ALU.add),
                 reads=["PS2", "dtb"], writes=["dtx"])
            T.op("act", act(S["dte"][:, :nch, :nh], S["dtx"][:, :nch, :nh], AF.Exp), reads=["dtx"], writes=["dte"])
            T.op("act", act(S["dt"][:, :nch, :nh], S["dte"][:, :nch, :nh], AF.Ln, bias=1.0), reads=["dte"], writes=["dt"])
            if vidx0 is not None:
                for c in range(nch):
                    T.op("dve", ts(S["dtv"][:, c, :nh], S["dt"][:, c, :nh], valid_bc[:, vidx0 + c:vidx0 + c + 1], None, ALU.mult),
                         reads=["dt", "valid"], writes=["dtv"])
                dtv = S["dtv"]; dtvn = "dtv"
            else:
                dtv = S["dt"]; dtvn = "dt"
            T.op("dve", tt(S["dA"][:, :nch, :nh], dtv[:, :nch, :nh], A_bc[:, hs0:hs0 + nh].unsqueeze(1).broadcast_to([128, nch, nh]), ALU.mult),
                 reads=[dtvn, "A"], writes=["dA"])
            return dtv, dtvn

        def ssd_decay2(S, nch, nh, dtv, dtvn, acp=(3, 0)):
            p3 = PS[acp[0]]; p3n = f"PS{acp[0]}"; a0 = acp[1]
            for c in range(nch):
                for k, (m_, mn) in enumerate(((trile, "trile"), (trigt, "trigt"), (ones, "ones"))):
                    o0 = a0 + (c * 3 + k) * 32
                    T.op("pe", mm(p3[:, o0:o0 + nh], m_[:], S["dA"][:, c, :nh]), reads=["dA", mn], writes=[p3n])
            p3v = p3[:, a0:a0 + nch * 96].rearrange("p (c k h) -> p c k h", k=3, h=32)
            for c in range(nch):
                T.op("act", act(S["ea"][:, c, :, :nh], p3v[:, c, :, 0:nh], AF.Exp), reads=[p3n], writes=["ea"])
            T.op("dve", tt(S["coef"][:, :nch, :nh], dtv[:, :nch, :nh], S["ea"][:, :nch, 1, :nh], ALU.mult), reads=[dtvn, "ea"], writes=["coef"])

        def ssd_state_chunk(S, g, c, ho, nx, full=None, tp_bank=None, s_bank=None):
            for _ in ssd_state_chunk_gen(S, g, c, ho, nx, full, tp_bank, s_bank):
                pass

        def ssd_state_chunk_gen(S, g, c, ho, nx, full=None, tp_bank=None, s_bank=None, hmul_eng="dve"):
            tpi = tp_bank if tp_bank is not None else (4 if full is not None else 4 + rr("tp", 2))
            tpb = PSb[tpi]; tpn = f"PS{tpi}"
            T.multi("pe", [trp(tpb[:, j * 128:(j + 1) * 128], S["feat"][:, j, c * 128:(c + 1) * 128], idb[:]) for j in range(nx)],
                    reads=[f"feat{j}" for j in range(nx)] + ["idb"], writes=[tpn])
            xi = rr("xdd", 2); xdd = S["xdd"][xi]; btok = S["btok"][xi]
            x3 = tpb[:, 0:512].rearrange("p (h d) -> p h d", d=64)
            T.op("dve", tt(xdd[:].rearrange("p (h d) -> p h d", d=64), x3, S["coef"][:, c, ho:ho + 8].unsqueeze(2).broadcast_to([128, 8, 64]), ALU.mult),
                 reads=[tpn, "coef"], writes=[f"xdd{xi}"])
            T.op("dve", cp(btok[:], tpb[:, 512:640]), reads=[tpn], writes=[f"btok{xi}"])
            if full is not None:
                full(tpb, tpn, x3)
            yield
            si = s_bank if s_bank is not None else (5 if full is not None else 6 + rr("Sps", 2))
            psS = PS[si]; psn = f"PS{si}"
            T.op("pe", mm(psS[:, :], btok[:], xdd[:]), reads=[f"btok{xi}", f"xdd{xi}"], writes=[psn])
            h3 = H[:, g, :].rearrange("p (h d) -> p h d", d=64)
            T.op(hmul_eng, tt(S["htmp"][:].rearrange("p (h d) -> p h d", d=64), h3, S["ea"][:, c, 2, ho:ho + 8].unsqueeze(2).broadcast_to([128, 8, 64]), ALU.mult),
                 reads=[f"H{g}", "ea"], writes=["htmp"])
            T.op("dve", tt(H[:, g, :], S["htmp"][:], psS[:, :], ALU.add), reads=["htmp", psn], writes=[f"H{g}"])

        eps1 = LN_EPS / (ALPHA * ALPHA)
        with ExitStack() as ph:
            B = ffn_alloc(ph, "p", 2)
            S = ssd_alloc(ph, "p", 5)
            xin = [sbt(ph, f"pxin{i}", [128, 1024]) for i in range(2)]
            xres = xin
            xT = sbt(ph, "pxT", [128, 8, 512], BF16)
            hTp = [sbt(ph, f"phT{i}", [128, 8, 512], BF16) for i in range(2)]
            Wxb = sbt(ph, "pWxb", [128, 8, 4, 640], BF16)
            histP = sbt(ph, "phist", [128, 24, 3])
            load_wd(B, f1d)
            for g in range(4):
                T.dma("pool", Wxb[:, :, g, 0:512], win_v[:, :, OFF_XBC + g * 512:OFF_XBC + (g + 1) * 512], writes=[f"Wxb{g}"])
                T.dma("pool", Wxb[:, :, g, 512:640], win_v[:, :, OFF_XBC + 2048 + g * 128:OFF_XBC + 2048 + (g + 1) * 128], writes=[f"Wxb{g}"])
            T.op("dve", lambda e: e.memset(histP[:], 0.0), writes=[f"hP{i}" for i in range(24)])

            def x_load_T(src_rows_fn, nch):
                for c in range(nch):
                    xi = rr("xin", 2)
                    T.dma("sp", xin[xi][:], src_rows_fn(c), writes=[f"xin{xi}"])
                    for half in range(2):
                        pb = PS[2 + half]
                        T.multi("pe", [trp(pb[:, j * 128:(j + 1) * 128], xin[xi][:, (half * 4 + j) * 128:(half * 4 + j + 1) * 128], idf[:]) for j in range(4)],
                                reads=[f"xin{xi}", "idf"], writes=[f"PS{2 + half}"])
                        T.op("act", act(xT[:, half * 4:(half + 1) * 4, c * 128:(c + 1) * 128], pb[:, :].rearrange("p (j t) -> p j t", t=128), AF.Copy),
                             reads=[f"PS{2 + half}"], writes=[f"xT{c}"])

            def make_resid(src_rows_fn):
                def f(c):
                    xi = rr("xin", 2)
                    T.dma("sp", xres[xi][:], src_rows_fn(c), writes=[f"xin{xi}"])
                    return xres[xi], f"xin{xi}"
                return f

            n_pre_blocks = pre_ch // 4

            def state_units(bi, hTb, hnames):
                dtv, dtvn = ssd_decay(S, lambda kc, c: hTb[:, kc, c * 128:(c + 1) * 128], hnames, 4, 0, 32, bi * 4)
                yield
                for g in range(4):
                    ccs = [(j, g * 4 + j, (lambda kc, g=g, j=j: Wxb[:, kc, g, j * 128:(j + 1) * 128]), f"Wxb{g}") for j in range(4)]
                    ccs.append((4, 16 + g, (lambda kc, g=g: Wxb[:, kc, g, 512:640]), f"Wxb{g}"))
                    for cc in ccs:
                        ssd_front(S, lambda kc: hTb[:, kc, 0:512], hnames, 512, [cc], histP, "hP", praw=(6,))
                        yield
                    if g == 0:
                        ssd_decay2(S, 4, 32, dtv, dtvn, acp=(2, 128))
                        yield
                    for c in range(4):
                        ssd_state_chunk(S, g, c, g * 8, 5, tp_bank=3, s_bank=7)
                        yield

            pending = None
            late = None
            for pb_i in range(n_pre_blocks):
                rows = lambda c, b0=pb_i * 4: x_prefix[(b0 + c) * 128:(b0 + c + 1) * 128, :]
                x_load_T(rows, 4)
                hTb = hTp[pb_i % 2]; hnames = [f"hTp{pb_i % 2}_{c}" for c in range(4)]

                def tail(c, pre, pname, part, hTb=hTb, slot=pb_i % 2):
                    ln_tail(B, eps1, "hT", pre, pname, part, dst=hTb, dcol=c * 128, gcol=g1c, bcol=b1c, gname=["g1c", "b1c"], dres=f"hTp{slot}_{c}")

                late = ffn_block(B, f1g, f1u, 4, lambda kc: xT[:, kc, 0:512], [f"xT{c}" for c in range(4)], ("dram", make_resid(rows)), eps1, tail,
                                 bg=pending, bg_per=2, after_first=late)
                pending = state_units(pb_i, hTb, hnames)
            if late is not None:
                late()
            if pending is not None:
                for _ in pending:
                    pass
            T.barrier()
        if upto in ("pre_ffn", "pre", "pre_x", "pre_gu", "pre_dn"):
            if upto == "pre":
                dump("H", H[:].rearrange("p g f -> p (g f)"), [128, 2048], ["H0", "H1", "H2", "H3"])
            T.final_wait("sp")
            return nc, dbg_outs
        if "H" in dbg:
            dump("H", H[:].rearrange("p g f -> p (g f)"), [128, 2048], ["H0", "H1", "H2", "H3"])

        main = st
        h1T = sbt(main, "h1T", [128, 8, (own_ch + 1) * 128], BF16)
        for g in range(4):
            T.op("act", act(Hbf[:, g, :], H[:, g, :], AF.Copy), reads=[f"H{g}"], writes=[f"Hbf{g}"])

        with ExitStack() as ph:
            B = ffn_alloc(ph, "a")
            xin = [sbt(ph, f"axin{i}", [128, 1024]) for i in range(2)]
            xres = xin
            xT = sbt(ph, "axT", [128, 8, 512], BF16)
            load_wd(B, f1d)

            def own_rows(ci):
                return x_halo[:, :] if ci == 0 else x_own[(ci - 1) * 128:ci * 128, :]

            def x_load_T2(c0, nch):
                for c in range(nch):
                    xi = rr("xin", 2)
                    T.dma("sp", xin[xi][:], own_rows(c0 + c), writes=[f"xin{xi}"])
                    for half in range(2):
                        pb = PS[2 + half]
                        T.multi("pe", [trp(pb[:, j * 128:(j + 1) * 128], xin[xi][:, (half * 4 + j) * 128:(half * 4 + j + 1) * 128], idf[:]) for j in range(4)],
                                reads=[f"xin{xi}", "idf"], writes=[f"PS{2 + half}"])
                        T.op("act", act(xT[:, half * 4:(half + 1) * 4, c * 128:(c + 1) * 128], pb[:, :].rearrange("p (j t) -> p j t", t=128), AF.Copy),
                             reads=[f"PS{2 + half}"], writes=[f"xT{c}"])

            late1 = None
            for (c0, nch) in _split_blocks(own_ch + 1):
                x_load_T2(c0, nch)

                def resid(c, c0=c0):
                    xi = rr("xin", 2)
                    T.dma("sp", xres[xi][:], own_rows(c0 + c), writes=[f"xin{xi}"])
                    return xres[xi], f"xin{xi}"

                def tail(c, pre, pname, part, c0=c0):
                    ln_tail(B, eps1, "hT", pre, pname, part, dst=h1T, dcol=(c0 + c) * 128, gcol=g1c, bcol=b1c, gname=["g1c", "b1c"], dres=f"h1T{c0 + c}")

                late1 = ffn_block(B, f1g, f1u, nch, lambda kc, nch=nch: xT[:, kc, 0:nch * 128], [f"xT{c}" for c in range(nch)], ("dram", resid), eps1, tail,
                                  after_first=late1)
            late1()
            T.barrier()
        if "h1T" in dbg:
            dump("h1T", h1T[:].rearrange("p k t -> p (k t)"), [128, 8 * (own_ch + 1) * 128], [f"h1T{i}" for i in range(own_ch + 1)])
        if upto == "ffn1":
            T.final_wait("sp")
            return nc, dbg_outs

        p2 = ExitStack()
        accA = sbt(p2, "accA", [128, own_ch, 1024], BF16)
        with ExitStack() as ph:
            Wq = sbt(ph, "Wq", [128, 8, 1024], BF16); Wk2 = sbt(ph, "Wk2", [128, 8, 4, 128], BF16); Wv = sbt(ph, "Wv", [128, 8, 256], BF16)
            Wga = sbt(ph, "Wga", [128, 8, 1024], BF16); Wab = sbt(ph, "Wab", [128, 8, 1024], BF16)
            kT2 = [sbt(ph, f"kT2_{i}", [128, 4, 128], BF16) for i in range(2)]
            Va = [sbt(ph, f"Va{i}", [128, 4, 65], BF16) for i in range(2)]
            qT = sbt(ph, "qT", [128, 8, 128], BF16)
            PT = [sbt(ph, f"PT{i}", [128, 4, 128], BF16) for i in range(4)]
            den = sbt(ph, "den", [128, 4]); rden = sbt(ph, "rden", [128, 4])
            ya = sbt(ph, "ya", [128, 1024], BF16); yaT = sbt(ph, "yaT", [128, 8, 128], BF16); ga = sbt(ph, "ga", [128, 1024])
            T.dma("pool", Wq[:], win_v[:, :, OFF_Q:OFF_Q + 1024], writes=["Wq"])
            for g in range(4):
                T.dma("pool", Wk2[:, :, g, 0:64], win_v[:, :, OFF_K + g * 64:OFF_K + (g + 1) * 64], writes=["Wk2"])
                T.dma("pool", Wk2[:, :, g, 64:128], win_v[:, :, OFF_K + g * 64:OFF_K + (g + 1) * 64], writes=["Wk2"])
            T.dma("pool", Wv[:], win_v[:, :, OFF_V:OFF_V + 256], writes=["Wv"])
            T.dma("pool", Wga[:], win_v[:, :, OFF_GA:OFF_GA + 1024], writes=["Wga"])
            T.dma("pool", Wab[:], w_ab.rearrange("(kc p) n -> p kc n", p=128), writes=["Wab"])
            for i in range(2):
                T.op("dve", lambda e, i=i: e.memset(Va[i][:, :, 64:65], 1.0), writes=[f"Va{i}one"])
            for ci in range(own_ch + 1):
                hc = slice(ci * 128, (ci + 1) * 128); hres = [f"h1T{ci}"]
                s_ = ci % 2
                pk = PS[0]
                for g in range(4):
                    T.multi("pe", [mm(pk[:, g * 128:(g + 1) * 128], Wk2[:, kc, g, :], h1T[:, kc, hc], kc == 0, kc == 7) for kc in range(8)],
                            reads=hres + ["Wk2"], writes=["PS0"])
                T.op("act", act(kT2[s_][:].rearrange("p g t -> p (g t)"), pk[:, :], AF.Copy), reads=["PS0"], writes=[f"kT2_{s_}"])
                pv = PS[1]
                T.multi("pe", [mm(pv[:, 0:256], h1T[:, kc, hc], Wv[:, kc, :], kc == 0, kc == 7) for kc in range(8)], reads=hres + ["Wv"], writes=["PS1"])
                T.op("act", act(Va[s_][:, :, 0:64], pv[:, 0:256].rearrange("p (g d) -> p g d", d=64), AF.Copy), reads=["PS1"], writes=[f"Va{s_}"])
                import os as _os
                _al = int(_os.environ.get("ALVL", "9"))
                if ci == 0 or _al < 2:
                    continue
                c = ci - 1
                for hb in range(2):
                    T.multi("pe", [mm(PS[2 + hb][:, j * 128:(j + 1) * 128], Wq[:, kc, (hb * 4 + j) * 128:(hb * 4 + j + 1) * 128], h1T[:, kc, hc], kc == 0, kc == 7)
                                   for j in range(4) for kc in range(8)], reads=hres + ["Wq"], writes=[f"PS{2 + hb}"])
                    T.op("act", act(qT[:, hb * 4:(hb + 1) * 4, :], PS[2 + hb][:, :].rearrange("p (j t) -> p j t", t=128), AF.Copy),
                         reads=[f"PS{2 + hb}"], writes=[f"qT{hb}"])
                for hb in range(2):
                    T.multi("pe", [mm(PS[hb][:, :], h1T[:, kc, hc], Wga[:, kc, hb * 512:(hb + 1) * 512], kc == 0, kc == 7) for kc in range(8)],
                            reads=hres + ["Wga"], writes=[f"PS{hb}"])
                    T.op("act", act(ga[:, hb * 512:(hb + 1) * 512], PS[hb][:, :], AF.Sigmoid), reads=[f"PS{hb}"], writes=[f"ga{hb}"])
                for g in range(4 if _al >= 3 else 0):
                    ptn = []
                    for half in range(2):
                        pS = PS[4 + half]
                        T.multi("pe", [mm(pS[:, kbi * 256:(kbi + 1) * 256], kT2[ks][half * 64:(half + 1) * 64, g, :],
                                          qT[half * 64:(half + 1) * 64, 2 * g:2 * g + 2, :], True, True) for kbi, ks in enumerate((1 - s_, s_))],
                                reads=["kT2_0", "kT2_1", "qT0", "qT1"], writes=[f"PS{4 + half}"])
                        pi = 2 * (g % 2) + half
                        pt = PT[pi]; ptn.append(f"PT{pi}")
                        T.op("act", act(pt[:].rearrange("p h t -> p (h t)"), pS[:, :], AF.Exp, scale=float(HEAD_SCALE)), reads=[f"PS{4 + half}"], writes=[f"PT{pi}"])
                        mpv, mpn = (mprev, "mprev") if c == 0 else (mprevf, "mprevf")
                        T.op("dve", tt(pt[:, 0:2, :], pt[:, 0:2, :], mpv[:].unsqueeze(1).broadcast_to([128, 2, 128]), ALU.mult),
                             reads=[f"PT{pi}", mpn], writes=[f"PT{pi}"])
                        T.op("dve", tt(pt[:, 2:4, :], pt[:, 2:4, :], mcur[:].unsqueeze(1).broadcast_to([128, 2, 128]), ALU.mult),
                             reads=[f"PT{pi}", "mcur"], writes=[f"PT{pi}"])
                    if _al < 4:
                        continue
                    po = PS[6]
                    fns = []
                    for hl in range(4):
                        half, jj = divmod(hl, 2)
                        for kbi, ks in enumerate((1 - s_, s_)):
                            fns.append(mm(po[:, hl * 65:(hl + 1) * 65], PT[2 * (g % 2) + half][:, kbi * 2 + jj, :], Va[ks][:, g, :], kbi == 0, kbi == 1))
                    T.multi("pe", fns, reads=ptn + ["Va0", "Va1", "Va0one", "Va1one"], writes=["PS6"])
                    if _al < 5:
                        continue
                    pov = po[:, 0:260].rearrange("p (h d) -> p h d", d=65)
                    T.op("dve", tt(den[:].rearrange("p (half jj) -> p half jj", jj=2), pov[:, :, 64].rearrange("p (half jj) -> p half jj", jj=2),
                                   esink_bc[:, 4 * g:4 * g + 4].rearrange("p (jj half) -> p half jj", half=2), ALU.add),
                         reads=["PS6", "esink"], writes=["den"])
                    T.op("dve", lambda e: e.reciprocal(out=rden[:], in_=den[:]), reads=["den"], writes=["rden"])
                    T.op("dve", tt(ya[:, g * 256:(g + 1) * 256].rearrange("p (jj half d) -> p half jj d", jj=2, half=2, d=64),
                                   pov[:, :, 0:64].rearrange("p (half jj) d -> p half jj d", jj=2),
                                   rden[:].rearrange("p (half jj) -> p half jj", jj=2).unsqueeze(3).broadcast_to([128, 2, 2, 64]), ALU.mult),
                         reads=["PS6", "rden"], writes=["ya"])
                if _al < 6:
                    continue
                T.multi("pe", [trp(PSb[7][:, j * 128:(j + 1) * 128], ya[:, j * 128:(j + 1) * 128], idb[:]) for j in range(8)], reads=["ya", "idb"], writes=["PS7"])
                T.op("dve", cp(yaT[:].rearrange("p k t -> p (k t)"), PSb[7][:, 0:1024]), reads=["PS7"], writes=["yaT"])
                for hb in range(2):
                    T.multi("pe", [mm(PS[2 + hb][:, :], yaT[:, kc, :], Wab[:, kc, hb * 512:(hb + 1) * 512], kc == 0, kc == 7) for kc in range(8)],
                            reads=["yaT", "Wab"], writes=[f"PS{2 + hb}"])
                    T.op("dve", stt(accA[:, c, hb * 512:(hb + 1) * 512], PS[2 + hb][:, :], 1.0 / ALPHA, ga[:, hb * 512:(hb + 1) * 512], ALU.mult, ALU.mult),
                         reads=[f"PS{2 + hb}", f"ga{hb}"], writes=[f"accA{c}"])
            T.barrier()
        if upto == "A":
            T.final_wait("sp")
            return nc, dbg_outs
        if "accA" in dbg:
            dump("accA", accA[:].rearrange("p c d -> p (c d)"), [128, own_ch * 1024], [])

        ys = sbt(p2, "ysacc", [128, own_ch, 1024], BF16)
        with ExitStack() as ph:
            S = ssd_alloc(ph, "c", 6)
            Wgrp = sbt(ph, "Wgrp", [128, 8, 1280], BF16)
            Wssm = sbt(ph, "Wssm", [128, 4, 1024], BF16)
            histO = sbt(ph, "histO", [128, 24, 3])
            zs = [sbt(ph, f"zs{i}", [128, 512]) for i in range(2)]; xdt = sbt(ph, "xdt", [128, 512], BF16); xst = sbt(ph, "xst", [128, 512], BF16)
            cbm = sbt(ph, "cbm", [128, 128], BF16); Rr = [sbt(ph, f"Rr{i}", [128, 8, 128]) for i in range(2)]; Lm = sbt(ph, "Lm", [128, 8, 128], BF16); Mm = sbt(ph, "Mm", [128, 8, 128], BF16)
            y1 = [sbt(ph, f"y1_{i}", [128, 512]) for i in range(2)]; y2 = sbt(ph, "y2", [128, 512]); ssq = sbt(ph, "ssq", [128, 8])
            un = sbt(ph, "un", [128, 512], BF16); unT = sbt(ph, "unT", [128, 4, 128], BF16)
            wsb_v = w_sb.rearrange("(kc p) n -> p kc n", p=128)
            v3 = lambda t_: t_[:].rearrange("p (h d) -> p h d", d=64)
            for g in range(4):
                T.dma("pool", Wgrp[:, :, 0:512], win_v[:, :, OFF_XBC + g * 512:OFF_XBC + (g + 1) * 512], writes=["Wgrp"])
                T.dma("pool", Wgrp[:, :, 512:640], win_v[:, :, OFF_XBC + 2048 + g * 128:OFF_XBC + 2048 + (g + 1) * 128], writes=["Wgrp"])
                T.dma("pool", Wgrp[:, :, 640:768], win_v[:, :, OFF_XBC + 2560 + g * 128:OFF_XBC + 2560 + (g + 1) * 128], writes=["Wgrp"])
                T.dma("pool", Wgrp[:, :, 768:1280], win_v[:, :, OFF_Z + g * 512:OFF_Z + (g + 1) * 512], writes=["Wgrp"])
                T.dma("pool", Wssm[:], wsb_v[:, g * 4:(g + 1) * 4, :], writes=["Wssm0"])
                for kc in range(4):
                    T.op("dve", ts(Wssm[:, kc, :], Wssm[:, kc, :], normw[:, g * 4 + kc:g * 4 + kc + 1], None, ALU.mult), reads=["Wssm0", "normw"], writes=["Wssm"])
                ccs = [(j, g * 4 + j, (lambda kc, j=j: Wgrp[:, kc, j * 128:(j + 1) * 128]), "Wgrp") for j in range(4)]
                ccs.append((4, 16 + g, (lambda kc: Wgrp[:, kc, 512:640]), "Wgrp"))
                ccs.append((5, 20 + g, (lambda kc: Wgrp[:, kc, 640:768]), "Wgrp"))
                ssd_front(S, lambda kc: h1T[:, kc, 0:128], ["h1T0"], 128, ccs, histO, "hO", only_hist=True)
                for (t0c, ntc) in _split_blocks(own_ch):
                    tok0 = (1 + t0c) * 128; N = ntc * 128
                    hT_reads = [f"h1T{1 + t0c + c}" for c in range(ntc)]
                    dtv, dtvn = ssd_decay(S, lambda kc, c: h1T[:, kc, tok0 + c * 128:tok0 + (c + 1) * 128], hT_reads, ntc, g * 8, 8, None)
                    ssd_front(S, lambda kc: h1T[:, kc, tok0:tok0 + N], hT_reads, N, ccs, histO, "hO")
                    ssd_decay2(S, ntc, 8, dtv, dtvn)

                    def mk_rr(c):
                        T.op("pool", tt(Rr[c % 2][:], trile[:].unsqueeze(1).broadcast_to([128, 8, 128]), S["dA"][:, c, 0:8].unsqueeze(2).broadcast_to([128, 8, 128]), ALU.mult),
                             reads=["trile", "dA"], writes=[f"Rr{c % 2}"])

                    mk_rr(0)

                    def back_dve(c):
                        b2 = c % 2
                        T.op("dve", lambda e: e.reciprocal(out=ssq[:, b2 * 4 + 2:b2 * 4 + 3], in_=ssq[:, b2 * 4 + 1:b2 * 4 + 2]), reads=[f"ssq{b2}b"], writes=[f"ssq{b2}c"])
                        T.op("dve", ts(un[:], y1[b2][:], ssq[:, b2 * 4 + 2:b2 * 4 + 3], None, ALU.mult), reads=[f"y1{b2}", f"ssq{b2}c"], writes=["un"])

                    def front(c, g=g, t0c=t0c, tok0=tok0, dtv=dtv, dtvn=dtvn, ntc=ntc):
                        cg = t0c + c; b2 = c % 2
                        hc = slice(tok0 + c * 128, tok0 + (c + 1) * 128); csl = slice(c * 128, (c + 1) * 128)
                        T.op("pe", mm(PS[2][:, 256:384], S["feat"][:, 4, csl], S["feat"][:, 5, csl]), reads=["feat4", "feat5"], writes=["PS2"])
                        T.op("dve", tt(cbm[:], PS[2][:, 256:384], trile[:], ALU.mult), reads=["PS2", "trile"], writes=["cbm"])
                        for hb in range(2):
                            T.op("pe", mm(PS[6 + hb][:, :], trigt[:], Rr[b2][:, hb * 4:(hb + 1) * 4, :]), reads=["trigt", f"Rr{b2}"], writes=[f"PS{6 + hb}"])
                            T.op("act", act(Lm[:, hb * 4:(hb + 1) * 4, :], PS[6 + hb][:, :].rearrange("p (h t) -> p h t", t=128), AF.Exp),
                                 reads=[f"PS{6 + hb}"], writes=[f"Lm{hb}"])
                        T.op("pool", tt(Mm[:], Lm[:], cbm[:].unsqueeze(1).broadcast_to([128, 8, 128]), ALU.mult), reads=["Lm0", "Lm1", "cbm"], writes=["Mm"])
                        if c + 1 < ntc:
                            mk_rr(c + 1)
                        T.multi("pe", [mm(PS[5][:, :], h1T[:, kc, hc], Wgrp[:, kc, 768:1280], kc == 0, kc == 7) for kc in range(8)],
                                reads=[f"h1T{1 + cg}", "Wgrp"], writes=["PS5"])
                        T.op("act", act(zs[b2][:], PS[5][:, :], AF.Silu), reads=["PS5"], writes=[f"zs{b2}"])
                        hin, hinn = (Hbf[:, g, :], f"Hbf{g}") if cg % 2 == 0 else (Hbf2[:, :], "Hbf2")
                        hout, houtn = (Hbf2[:, :], "Hbf2") if cg % 2 == 0 else (Hbf[:, g, :], f"Hbf{g}")

                        def full(tpb, tpn, x3, c=c):
                            T.op("dve", tt(v3(xdt), x3, dtv[:, c, 0:8].unsqueeze(2).broadcast_to([128, 8, 64]), ALU.mult), reads=[tpn, dtvn], writes=["xdt"])
                            T.op("dve", cp(xst[:], tpb[:, 0:512]), reads=[tpn], writes=["xst"])
                            if c >= 1:
                                back_dve(c - 1)

                        for _ in ssd_state_chunk_gen(S, g, c, 0, 5, full, None, None, "pool"):
                            pass
                        T.op("act", act(hout, H[:, g, :], AF.Copy), reads=[f"H{g}"], writes=[houtn])
                        T.op("pe", mm(PS[7][:, :], S["feat"][:, 5, csl], hin), reads=["feat5", hinn], writes=["PS7"])
                        T.multi("pe", [mm(PS[6][:, h * 64:(h + 1) * 64], Mm[:, h, :], xdt[:, h * 64:(h + 1) * 64]) for h in range(8)],
                                reads=["Mm", "xdt"], writes=["PS6"])
                        yy = y1[b2]; yn_ = f"y1{b2}"
                        T.op("dve", tt(v3(yy), PS[7][:, :].rearrange("p (h d) -> p h d", d=64), S["ea"][:, c, 0, 0:8].unsqueeze(2).broadcast_to([128, 8, 64]), ALU.mult),
                             reads=["PS7", "ea"], writes=[yn_])
                        T.op("dve", tt(yy[:], yy[:], PS[6][:, :], ALU.add), reads=[yn_, "PS6"], writes=[yn_])
                        T.op("pool", tt(v3(y2), v3(xst), dsk_bc[:, g * 8:(g + 1) * 8].unsqueeze(2).broadcast_to([128, 8, 64]), ALU.mult), reads=["xst", "dsk"], writes=["y2"])
                        T.op("pool", tt(yy[:], yy[:], y2[:], ALU.add), reads=[yn_, "y2"], writes=[yn_])
                        T.op("pool", tt(yy[:], yy[:], zs[b2][:], ALU.mult), reads=[yn_, f"zs{b2}"], writes=[yn_])
                        T.op("act", act(y2[:], yy[:], AF.Square, accum_out=ssq[:, b2 * 4:b2 * 4 + 1]), reads=[yn_, "y2"], writes=["y2", f"ssq{b2}a"])
                        T.op("act", act(ssq[:, b2 * 4 + 1:b2 * 4 + 2], ssq[:, b2 * 4:b2 * 4 + 1], AF.Sqrt, scale=1.0 / 512.0, bias=float(RMS_EPS)), reads=[f"ssq{b2}a"], writes=[f"ssq{b2}b"])

                    def back(c, g=g, t0c=t0c, ntc=ntc):
                        cg = t0c + c; b2 = c % 2
                        if c == ntc - 1:
                            back_dve(c)
                        T.multi("pe", [trp(PSb[4][:, j * 128:(j + 1) * 128], un[:, j * 128:(j + 1) * 128], idb[:]) for j in range(4)], reads=["un", "idb"], writes=["PS4"])
                        T.op("dve", cp(unT[:].rearrange("p k t -> p (k t)"), PSb[4][:, 0:512]), reads=["PS4"], writes=["unT"])
                        for hb in range(2):
                            T.multi("pe", [mm(PS[hb][:, :], unT[:, kc, :], Wssm[:, kc, hb * 512:(hb + 1) * 512], kc == 0, kc == 3) for kc in range(4)],
                                    reads=["unT", "Wssm"], writes=[f"PS{hb}"])
                            if g == 0:
                                T.op("act", act(ys[:, cg, hb * 512:(hb + 1) * 512], PS[hb][:, :], AF.Copy), reads=[f"PS{hb}"], writes=[f"ys{cg}_{hb}"])
                            else:
                                T.op("dve", tt(ys[:, cg, hb * 512:(hb + 1) * 512], ys[:, cg, hb * 512:(hb + 1) * 512], PS[hb][:, :], ALU.add),
                                     reads=[f"PS{hb}", f"ys{cg}_{hb}"], writes=[f"ys{cg}_{hb}"])

                    for c in range(ntc + 1):
                        if c < ntc:
                            front(c)
                        if c >= 1:
                            back(c - 1)
            T.barrier()
        if upto == "C":
            T.final_wait("sp")
            return nc, dbg_outs
        if "ys" in dbg:
            dump("ys", ys[:].rearrange("p c d -> p (c d)"), [128, own_ch * 1024], [])

        with ExitStack() as ph:
            Wgs = sbt(ph, "Wgs", [128, 8, 1024], BF16); Wo = sbt(ph, "Wo", [128, 8, 1024], BF16)
            gs = [sbt(ph, f"gs{i}", [128, 1024]) for i in range(2)]; tmpm = [sbt(ph, f"tmpm{i}", [128, 1024]) for i in range(2)]
            mrg = [sbt(ph, f"mrg{i}", [128, 1024], BF16) for i in range(2)]; mT = [sbt(ph, f"mT{i}", [128, 8, 128], BF16) for i in range(2)]
            B2 = {"pre": [sbt(ph, f"dpre{i}", [128, 1024]) for i in range(2)], "st6": sbt(ph, "dst6", [128, 2, 6]), "mv": sbt(ph, "dmv", [128, 4])}
            T.dma("pool", Wgs[:], win_v[:, :, OFF_GS:OFF_GS + 1024], writes=["Wgs"])
            T.dma("pool", Wo[:], w_o.rearrange("(kc p) n -> p kc n", p=128), writes=["Wo"])

            def d_s1(c):
                b = c % 2
                hc = slice((c + 1) * 128, (c + 2) * 128); hres = [f"h1T{c + 1}"]
                for hb in range(2):
                    T.multi("pe", [mm(PS[hb][:, :], h1T[:, kc, hc], Wgs[:, kc, hb * 512:(hb + 1) * 512], kc == 0, kc == 7) for kc in range(8)],
                            reads=hres + ["Wgs"], writes=[f"PS{hb}"])
                    T.op("act", act(gs[b][:, hb * 512:(hb + 1) * 512], PS[hb][:, :], AF.Sigmoid), reads=[f"PS{hb}"], writes=[f"gs{b}_{hb}"])
                    T.op("dve", stt(tmpm[b][:, hb * 512:(hb + 1) * 512], ys[:, c, hb * 512:(hb + 1) * 512], 1.0 / ALPHA, gs[b][:, hb * 512:(hb + 1) * 512], ALU.mult, ALU.mult),
                         reads=[f"gs{b}_{hb}"], writes=[f"tmpm{b}_{hb}"])
                    T.op("pool", tt(mrg[b][:, hb * 512:(hb + 1) * 512], tmpm[b][:, hb * 512:(hb + 1) * 512], accA[:, c, hb * 512:(hb + 1) * 512], ALU.add),
                         reads=[f"tmpm{b}_{hb}"], writes=[f"mrg{b}_{hb}"])
                T.multi("pe", [trp(PSb[4][:, j * 128:(j + 1) * 128], mrg[b][:, j * 128:(j + 1) * 128], idb[:]) for j in range(8)],
                        reads=[f"mrg{b}_0", f"mrg{b}_1", "idb"], writes=["PS4"])
                T.op("dve", cp(mT[b][:].rearrange("p k t -> p (k t)"), PSb[4][:, 0:1024]), reads=["PS4"], writes=[f"mT{b}"])

            def d_s2(c):
                b = c % 2
                hc = slice((c + 1) * 128, (c + 2) * 128); hres = [f"h1T{c + 1}"]
                for hb in range(2):
                    fns = [mm(PS[6 + hb][:, :], mT[b][:, kc, :], Wo[:, kc, hb * 512:(hb + 1) * 512], kc == 0, False) for kc in range(8)]
                    fns += [mm(PS[6 + hb][:, j * 128:(j + 1) * 128], h1T[:, hb * 4 + j, hc], idb[:], False, j == 3) for j in range(4)]
                    T.multi("pe", fns, reads=[f"mT{b}", "Wo", "idb"] + hres, writes=[f"PS{6 + hb}"])
                    T.op("act", act(B2["pre"][b][:, hb * 512:(hb + 1) * 512], PS[6 + hb][:, :], AF.Copy),
                         reads=[f"PS{6 + hb}", f"pre{b}a", f"pre{b}b"], writes=[f"pre{b}" + ("a" if hb == 0 else "b")])
                ln_tail(B2, eps1, "hT", B2["pre"][b], f"pre{b}", "stats")

            def d_s3(c):
                b = c % 2
                ln_tail(B2, eps1, "hT", B2["pre"][b], f"pre{b}", "rest", dst=h1T, dcol=c * 128, gcol=g2c, bcol=b2c, gname=["g2c", "b2c"], dres=f"h1T{c}")

            for step in range(own_ch + 2):
                if step < own_ch:
                    d_s1(step)
                if 1 <= step <= own_ch:
                    d_s2(step - 1)
                if step >= 2:
                    d_s3(step - 2)
            T.barrier()
        p2.close()
        if upto == "D":
            T.final_wait("sp")
            return nc, dbg_outs
        if "x2T" in dbg:
            dump("x2T", h1T[:, :, 0:own_ch * 128], [128, 8, own_ch * 128], [])

        with ExitStack() as ph:
            B = ffn_alloc(ph, "f", 3)
            g3 = sbt(ph, "g3", [128, 1024]); b3 = sbt(ph, "b3", [128, 1024]); yo = [sbt(ph, f"yo{i}", [128, 1024]) for i in range(2)]
            load_wd(B, f2d)
            T.dma("sp", g3[:], ln3_gr[0, :].partition_broadcast(128), writes=["g3"])
            T.dma("sp", b3[:], ln3_br[0, :].partition_broadcast(128), writes=["b3"])
            late2 = None
            for (c0, nch) in _split_blocks(own_ch):
                x2r = [f"h1T{c0 + c}" for c in range(nch)]

                def tail(c, pre, pname, part, c0=c0):
                    if part == "stats":
                        ln_tail(B, eps1, "out", pre, pname, "stats")
                        return
                    yi = rr("yo", 2)
                    ln_tail(B, eps1, "out", pre, pname, part, g_bc=g3, b_bc=b3, orow=(c0 + c) * 128, yo=yo[yi], yoname=f"yo{yi}")

                late2 = ffn_block(B, f2g, f2u, nch, lambda kc, c0=c0, nch=nch: h1T[:, kc, c0 * 128:(c0 + nch) * 128], x2r,
                                  ("ident", (lambda k, c, c0=c0: h1T[:, k, (c0 + c) * 128:(c0 + c + 1) * 128]), x2r), eps1, tail, after_first=late2)
            late2()
        T.final_wait("sp")
    return nc, dbg_outs


def _make_in_maps(inp, own_ch, pre_ch):
    x = np.asarray(inp["x"], np.float32)
    batch, seq, _ = x.shape
    own = own_ch * CH
    npos = seq // own
    sq = lambda k: np.ascontiguousarray(np.asarray(inp[k], np.float32)[0])
    col = lambda v: np.ascontiguousarray(v.reshape(-1, 128).T)
    shared = dict(_const_inputs())
    shared.update({
        "ffn1_wg": sq("ffn1_w_gate"), "ffn1_wu": sq("ffn1_w_up"), "ffn1_wd": sq("ffn1_w_down"),
        "ffn2_wg": sq("ffn2_w_gate"), "ffn2_wu": sq("ffn2_w_up"), "ffn2_wd": sq("ffn2_w_down"),
        "w_in": sq("w_in"), "w_ab": sq("w_attn_branch"), "w_sb": sq("w_ssm_branch"), "w_o": sq("w_out"),
        "ln1_gc": col(sq("ln1_g")), "ln1_bc": col(sq("ln1_b")), "ln2_gc": col(sq("ln2_g")), "ln2_bc": col(sq("ln2_b")),
        "ln3_gr": sq("ln3_g").reshape(1, -1), "ln3_br": sq("ln3_b").reshape(1, -1),
        "conv_wc": np.ascontiguousarray(sq("conv_w").reshape(4, 24, 128).transpose(2, 1, 0).reshape(128, 96)),
        "conv_bc": col(sq("conv_b")),
        "dtb_r": sq("dt_bias").reshape(1, -1), "alog_r": sq("a_log").reshape(1, -1), "dsk_r": sq("d_skip").reshape(1, -1),
        "sinks_r": sq("sinks").reshape(1, -1), "normw_c": col(sq("ssm_norm_w")),
    })
    maps = []
    for c in range(batch * npos):
        b, p = divmod(c, npos)
        t0 = p * own
        m = dict(shared)
        m["x_own"] = np.ascontiguousarray(x[b, t0:t0 + own])
        halo = np.zeros((HALO, D), np.float32)
        if p > 0:
            halo[:] = x[b, t0 - HALO:t0]
        m["x_halo"] = halo
        pre = np.zeros((pre_ch * CH, D), np.float32)
        if p > 0:
            pre[:t0] = x[b, :t0]
        m["x_prefix"] = pre
        pv = np.zeros((1, pre_ch), np.float32); pv[0, :t0 // CH] = 1.0
        m["pre_valid"] = pv
        m["has_prev"] = np.full((1, 1), 1.0 if p > 0 else 0.0, np.float32)
        maps.append(m)
    return maps


def kernel(**inputs):
    maps = _make_in_maps(inputs, N_OWN_CH, N_PRE_CH)
    nc = build(N_OWN_CH, N_PRE_CH)
    if isinstance(nc, tuple):
        nc = nc[0]
    res = run_bass_kernel_spmd(nc, maps, core_ids=list(range(NCORES)))
    x = np.asarray(inputs["x"])
    outp = np.empty(x.shape, np.float32)
    npos = SEQ // OWN
    for c in range(NCORES):
        b, p = divmod(c, npos)
        outp[b, p * OWN:(p + 1) * OWN] = res.results[c]["out"]
    return outp
```

```python
import numpy as np
from contextlib import ExitStack
import concourse.bass as bass
import concourse.mybir as mybir
from concourse.bass_utils import run_bass_kernel_spmd

D = 1024; DFF = 2816; NCORES = 8; SEQ = 8192; BATCH = 2
OWN = 2048; CH = 128; N_OWN_CH = OWN // CH
HALO = 128
PREFIX_MAX = SEQ - OWN; N_PRE_CH = PREFIX_MAX // CH
ALPHA = 2.0 ** 0.25; LN_EPS = 1e-5; RMS_EPS = 1e-5; HEAD_SCALE = 64 ** -0.5
LABEL_PREFIX = "no x-core exchange avail -> redundant SSD prefix recompute (~2.1x FLOPs/core); "


F32 = mybir.dt.float32
BF16 = mybir.dt.bfloat16
AF = mybir.ActivationFunctionType
ALU = mybir.AluOpType
AX = mybir.AxisListType


class Tracker:
    def __init__(self, nc, stack, n_dma_sems=40):
        self.nc = nc
        self.eng = {"pe": nc.tensor, "act": nc.scalar, "dve": nc.vector, "pool": nc.gpsimd, "sp": nc.sync}
        self.sem = {e: stack.enter_context(nc.semaphore(f"s_{e}")) for e in self.eng}
        self.cnt = {e: 0 for e in self.eng}
        self.dma_sems = [stack.enter_context(nc.semaphore(f"s_dma{i}")) for i in range(n_dma_sems)]
        self.dma_cnt = [0] * n_dma_sems
        self.dma_rr = 0
        self.known = {e: {} for e in self.eng}
        self.last_w = {}
        self.readers = {}
        self.n_waits = 0
        self.n_ops = 0

    def _wait(self, e, tok):
        sem, val, _src = tok
        k = self.known[e]
        if k.get(sem.name, 0) >= val:
            return
        self.eng[e].wait_ge(sem, val)
        k[sem.name] = val
        self.n_waits += 1

    def _deps(self, e, reads, writes, is_dma):
        raw, other = [], []
        for r in reads:
            w = self.last_w.get(r)
            if w is not None:
                raw.append(w)
        for w_ in writes:
            w = self.last_w.get(w_)
            if w is not None:
                other.append(w)
            other.extend(self.readers.get(w_, ()))
        for tok in raw:
            if tok[2] == e and e == "pe" and not is_dma:
                continue
            self._wait(e, tok)
        for tok in other:
            if tok[2] == e and not is_dma:
                continue
            self._wait(e, tok)

    def _register(self, tok, reads, writes):
        for r in reads:
            self.readers.setdefault(r, []).append(tok)
        for w_ in writes:
            self.last_w[w_] = tok
            self.readers[w_] = []

    def op(self, e, fn, reads=(), writes=()):
        self._deps(e, reads, writes, False)
        ins = fn(self.eng[e])
        self.cnt[e] += 1
        ins.then_inc(self.sem[e], 1)
        tok = (self.sem[e], self.cnt[e], e)
        self._register(tok, reads, writes)
        self.n_ops += 1
        return tok

    def multi(self, e, fns, reads=(), writes=()):
        self._deps(e, reads, writes, False)
        ins = None
        for fn in fns:
            ins = fn(self.eng[e])
        self.cnt[e] += 1
        ins.then_inc(self.sem[e], 1)
        tok = (self.sem[e], self.cnt[e], e)
        self._register(tok, reads, writes)
        self.n_ops += 1
        return tok

    def dma(self, q, out, in_, reads=(), writes=(), **kw):
        self._deps(q, reads, writes, True)
        i = self.dma_rr
        self.dma_rr = (self.dma_rr + 1) % len(self.dma_sems)
        sem = self.dma_sems[i]
        if self.dma_cnt[i] > 0:
            self._wait(q, (sem, 16 * self.dma_cnt[i], "dma"))
        self.dma_cnt[i] += 1
        ins = self.eng[q].dma_start(out=out, in_=in_, **kw)
        ins.then_inc(sem, 16)
        tok = (sem, 16 * self.dma_cnt[i], "dma")
        self._register(tok, reads, writes)
        self.n_ops += 1
        return tok

    def barrier(self):
        toks = [(self.sem[e], self.cnt[e], e) for e in self.eng if self.cnt[e] > 0]
        toks += [(s, 16 * c, "dma") for s, c in zip(self.dma_sems, self.dma_cnt) if c > 0]
        for e in self.eng:
            for tok in toks:
                if tok[2] == e:
                    continue
                self._wait(e, tok)
        self.last_w.clear()
        self.readers.clear()

    def final_wait(self, e="sp"):
        toks = [(self.sem[x], self.cnt[x], x) for x in self.eng if self.cnt[x] > 0 and x != e]
        toks += [(s, 16 * c, "dma") for s, c in zip(self.dma_sems, self.dma_cnt) if c > 0]
        for tok in toks:
            self._wait(e, tok)


def _core_layout(x):
    per_core = []
    for c in range(NCORES):
        b, p = divmod(c, SEQ // OWN)
        t0 = p * OWN
        own = np.ascontiguousarray(x[b, t0:t0 + OWN])
        halo = np.zeros((HALO, D), np.float32)
        if p > 0:
            halo[:] = x[b, t0 - HALO:t0]
        prefix = np.zeros((PREFIX_MAX, D), np.float32)
        if p > 0:
            prefix[:t0] = x[b, :t0]
        pre_valid = np.zeros((1, N_PRE_CH), np.float32)
        pre_valid[0, :t0 // CH] = 1.0
        has_prev = np.full((1, 1), 1.0 if p > 0 else 0.0, np.float32)
        per_core.append({"x_own": own, "x_halo": halo, "x_prefix": prefix,
                         "pre_valid": pre_valid, "has_prev": has_prev})
    return per_core


def _const_inputs():
    i = np.arange(CH)
    return {
        "c_ident": np.eye(CH, dtype=np.float32),
        "c_tri_le": (i[:, None] <= i[None, :]).astype(np.float32),
        "c_tri_gt": (i[:, None] > i[None, :]).astype(np.float32),
    }


OFF_Q = 0; OFF_K = 1024; OFF_V = 1280; OFF_Z = 1536; OFF_XBC = 3584; OFF_DT = 6656; OFF_GA = 6688; OFF_GS = 7712
NFF = DFF // 128


def _split_blocks(n, mx=4):
    nb = -(-n // mx)
    base, rem = divmod(n, nb)
    sizes = [base + (1 if i < rem else 0) for i in range(nb)]
    out, s = [], 0
    for z in sizes:
        out.append((s, z)); s += z
    return out


def build(own_ch=N_OWN_CH, pre_ch=N_PRE_CH, upto="all", dbg=()):
    nc = bass.Bass("TRN2", target_bir_lowering=False)
    NT_OWN = own_ch * CH
    NT_PRE = pre_ch * CH

    def din(name, shape):
        return nc.dram_tensor(name, list(shape), F32, kind="ExternalInput").ap()

    x_own = din("x_own", [NT_OWN, D]); x_halo = din("x_halo", [HALO, D]); x_prefix = din("x_prefix", [NT_PRE, D])
    pre_valid = din("pre_valid", [1, pre_ch]); has_prev = din("has_prev", [1, 1])
    c_ident = din("c_ident", [CH, CH]); c_tri_le = din("c_tri_le", [CH, CH]); c_tri_gt = din("c_tri_gt", [CH, CH])
    f1g = din("ffn1_wg", [D, DFF]); f1u = din("ffn1_wu", [D, DFF]); f1d = din("ffn1_wd", [DFF, D])
    f2g = din("ffn2_wg", [D, DFF]); f2u = din("ffn2_wu", [D, DFF]); f2d = din("ffn2_wd", [DFF, D])
    w_in = din("w_in", [D, 8736]); w_ab = din("w_ab", [D, D]); w_sb = din("w_sb", [2048, D]); w_o = din("w_o", [D, D])
    ln1_gc = din("ln1_gc", [128, 8]); ln1_bc = din("ln1_bc", [128, 8]); ln2_gc = din("ln2_gc", [128, 8]); ln2_bc = din("ln2_bc", [128, 8])
    ln3_gr = din("ln3_gr", [1, D]); ln3_br = din("ln3_br", [1, D])
    conv_wc = din("conv_wc", [128, 96]); conv_bc = din("conv_bc", [128, 24])
    dtb_r = din("dtb_r", [1, 32]); alog_r = din("alog_r", [1, 32]); dsk_r = din("dsk_r", [1, 32]); sinks_r = din("sinks_r", [1, 16])
    normw_c = din("normw_c", [128, 16])
    out = nc.dram_tensor("out", [NT_OWN, D], F32, kind="ExternalOutput").ap()
    dbg_outs = {}

    win_v = w_in.rearrange("(kc p) n -> p kc n", p=128)

    with ExitStack() as st:
        T = Tracker(nc, st, n_dma_sems=48)

        def sbt(stack, name, shape, dt=F32):
            return stack.enter_context(nc.sbuf_tensor(name, list(shape), dt))

        PS = [st.enter_context(nc.psum_tensor(f"PS{i}", [128, 512], F32)) for i in range(8)]
        PSb = [p[:].bitcast(BF16) for p in PS]
        st.enter_context(nc.Block())
        rrc = {}

        def rr(key, n):
            v = rrc.get(key, 0); rrc[key] = v + 1
            return v % n

        def mm(o, lhsT, rhs, start=True, stop=True):
            return lambda e: e.matmul(o, lhsT=lhsT, rhs=rhs, start=start, stop=stop)

        def trp(o, i, ident):
            return lambda e: e.transpose(out=o, in_=i, identity=ident)

        def act(o, i, func, **kw):
            return lambda e: e.activation(out=o, in_=i, func=func, **kw)

        def tt(o, a, b, op):
            return lambda e: e.tensor_tensor(out=o, in0=a, in1=b, op=op)

        def ts(o, a, s1, s2, op0, op1=None):
            if op1 is None:
                return lambda e: e.tensor_scalar(out=o, in0=a, scalar1=s1, scalar2=None, op0=op0)
            return lambda e: e.tensor_scalar(out=o, in0=a, scalar1=s1, scalar2=s2, op0=op0, op1=op1)

        def stt(o, a, s, b, op0, op1):
            return lambda e: e.scalar_tensor_tensor(out=o, in0=a, scalar=s, in1=b, op0=op0, op1=op1)

        def cp(o, i):
            return lambda e: e.tensor_copy(out=o, in_=i)

        def dump(name, ap, shape, reads):
            d = nc.dram_tensor("dbg_" + name, list(shape), F32, kind="ExternalOutput").ap()
            tmp = sbt(st, "dbgt_" + name, shape, F32)
            T.op("act", act(tmp[:], ap, AF.Copy), reads=reads, writes=["dbgt_" + name])
            T.dma("sp", d, tmp[:], reads=["dbgt_" + name], writes=["dbgd_" + name])
            dbg_outs[name] = d

        idf = sbt(st, "idf", [128, 128]); idb = sbt(st, "idb", [128, 128], BF16)
        trile = sbt(st, "trile", [128, 128]); trigt = sbt(st, "trigt", [128, 128]); ones = sbt(st, "ones", [128, 128])
        mcur = sbt(st, "mcur", [128, 128], BF16); mprev = sbt(st, "mprev", [128, 128], BF16); mprevf = sbt(st, "mprevf", [128, 128], BF16)
        hp_bc = sbt(st, "hp_bc", [128, 1]); valid_bc = sbt(st, "valid_bc", [128, pre_ch])
        dtb_bc = sbt(st, "dtb_bc", [128, 32]); A_bc = sbt(st, "A_bc", [128, 32]); dsk_bc = sbt(st, "dsk_bc", [128, 32])
        esink_bc = sbt(st, "esink_bc", [128, 16])
        cw = sbt(st, "cw", [128, 24, 4]); cb = sbt(st, "cb", [128, 24]); normw = sbt(st, "normw", [128, 16])
        g1c = sbt(st, "g1c", [128, 8]); b1c = sbt(st, "b1c", [128, 8]); g2c = sbt(st, "g2c", [128, 8]); b2c = sbt(st, "b2c", [128, 8])
        Wdt = sbt(st, "Wdt", [128, 8, 32], BF16)
        H = sbt(st, "Hst", [128, 4, 512]); Hbf = sbt(st, "Hbf", [128, 4, 512], BF16); Hbf2 = sbt(st, "Hbf2", [128, 512], BF16)

        T.dma("sp", idf[:], c_ident[:, :], writes=["idf"])
        T.dma("sp", trile[:], c_tri_le[:, :], writes=["trile"])
        T.dma("sp", trigt[:], c_tri_gt[:, :], writes=["trigt"])
        T.dma("sp", hp_bc[:], has_prev[0, :].partition_broadcast(128), writes=["hp"])
        T.dma("sp", valid_bc[:], pre_valid[0, :].partition_broadcast(128), writes=["valid"])
        T.dma("sp", dtb_bc[:], dtb_r[0, :].partition_broadcast(128), writes=["dtb"])
        T.dma("sp", A_bc[:], alog_r[0, :].partition_broadcast(128), writes=["A0"])
        T.dma("sp", dsk_bc[:], dsk_r[0, :].partition_broadcast(128), writes=["dsk"])
        T.dma("sp", esink_bc[:], sinks_r[0, :].partition_broadcast(128), writes=["esink0"])
        T.dma("sp", cw[:].rearrange("p c j -> p (c j)"), conv_wc[:, :], writes=["cw"])
        T.dma("sp", cb[:], conv_bc[:, :], writes=["cb"])
        T.dma("sp", normw[:], normw_c[:, :], writes=["normw"])
        for (t_, d_, n_) in ((g1c, ln1_gc, "g1c"), (b1c, ln1_bc, "b1c"), (g2c, ln2_gc, "g2c"), (b2c, ln2_bc, "b2c")):
            T.dma("sp", t_[:], d_[:, :], writes=[n_])
        T.dma("pool", Wdt[:], win_v[:, :, OFF_DT:OFF_DT + 32], writes=["Wdt"])
        T.op("dve", cp(idb[:], idf[:]), reads=["idf"], writes=["idb"])
        T.op("dve", lambda e: e.memset(ones[:], 1.0), writes=["ones"])
        T.op("dve", cp(mcur[:], trile[:]), reads=["trile"], writes=["mcur"])
        T.op("dve", ts(mprev[:], trigt[:], hp_bc[:, 0:1], None, ALU.mult), reads=["trigt", "hp"], writes=["mprev"])
        T.op("dve", cp(mprevf[:], trigt[:]), reads=["trigt"], writes=["mprevf"])
        T.op("act", act(A_bc[:], A_bc[:], AF.Exp), reads=["A0"], writes=["A1"])
        T.op("dve", ts(A_bc[:], A_bc[:], -1.0, None, ALU.mult), reads=["A1"], writes=["A"])
        T.op("act", act(esink_bc[:], esink_bc[:], AF.Exp), reads=["esink0"], writes=["esink"])
        T.op("dve", lambda e: e.memset(H[:], 0.0), writes=["H0", "H1", "H2", "H3"])

        if upto == "setup":
            dump("A", A_bc[:], [128, 32], ["A"])
            T.final_wait("sp")
            return nc, dbg_outs

        def ffn_alloc(ph, tag, nring=3):
            B = {"nring": nring}
            B["wg"] = [sbt(ph, f"{tag}wg{i}", [128, 8, 256], BF16) for i in range(nring)]
            B["wu"] = [sbt(ph, f"{tag}wu{i}", [128, 8, 256], BF16) for i in range(nring)]
            B["wd"] = sbt(ph, f"{tag}wd", [128, NFF, 1024], BF16)
            B["actT"] = sbt(ph, f"{tag}actT", [128, NFF, 512], BF16)
            B["sg"] = [sbt(ph, f"{tag}sg{i}", [128, 512]) for i in range(2)]
            B["pre"] = [sbt(ph, f"{tag}pre{i}", [128, 1024]) for i in range(2)]
            B["st6"] = sbt(ph, f"{tag}st6", [128, 2, 6]); B["mv"] = sbt(ph, f"{tag}mv", [128, 4])
            return B

        def load_wd(B, wd_dram):
            v = wd_dram.rearrange("(fc p) n -> p fc n", p=128)
            for f0 in range(0, NFF, 8):
                f1 = min(NFF, f0 + 8)
                T.dma("pool", B["wd"][:, f0:f1, :], v[:, f0:f1, :], writes=["wd"])

        def ln_tail(B, eps_eff, mode, pre, pname, part="all", **kw):
            st6, mv = B["st6"], B["mv"]
            pre_reads = [pname + "a", pname + "b"]
            if part in ("all", "stats"):
                ln_stats(B, eps_eff, pre, pre_reads)
            if part == "stats":
                return
            ln_rest(B, mode, pre, pre_reads, **kw)

        def ln_stats(B, eps_eff, pre, pre_reads):
            st6, mv = B["st6"], B["mv"]
            T.op("dve", lambda e: e.bn_stats(out=st6[:, 0, :], in_=pre[:, 0:512]), reads=pre_reads, writes=["st6a"])
            T.op("dve", lambda e: e.bn_stats(out=st6[:, 1, :], in_=pre[:, 512:1024]), reads=pre_reads, writes=["st6b"])
            T.op("dve", lambda e: e.bn_aggr(out=mv[:, 0:2], in_=st6[:].rearrange("p a b -> p (a b)")), reads=["st6a", "st6b"], writes=["mv01"])
            T.op("act", act(mv[:, 2:3], mv[:, 1:2], AF.Sqrt, bias=float(eps_eff)), reads=["mv01"], writes=["mv2"])
            T.op("dve", lambda e: e.reciprocal(out=mv[:, 3:4], in_=mv[:, 2:3]), reads=["mv2"], writes=["mv3"])
            T.op("dve", ts(pre[:], pre[:], mv[:, 0:1], mv[:, 3:4], ALU.subtract, ALU.mult), reads=pre_reads + ["mv01", "mv3"], writes=pre_reads)

        def ln_rest(B, mode, pre, pre_reads, **kw):
            if mode == "hT":
                dst, dcol, gcol, bcol, gname, dres = kw["dst"], kw["dcol"], kw["gcol"], kw["bcol"], kw["gname"], kw["dres"]
                for half in range(2):
                    pb = PS[2 + half]
                    T.multi("pe", [trp(pb[:, j * 128:(j + 1) * 128], pre[:, (half * 4 + j) * 128:(half * 4 + j + 1) * 128], idf[:]) for j in range(4)],
                            reads=pre_reads + ["idf"], writes=[f"PS{2 + half}"])
                    for j in range(4):
                        kc = half * 4 + j
                        T.op("dve", ts(dst[:, kc, dcol:dcol + 128], pb[:, j * 128:(j + 1) * 128], gcol[:, kc:kc + 1], bcol[:, kc:kc + 1], ALU.mult, ALU.add),
                             reads=[f"PS{2 + half}"] + gname, writes=[dres])
            else:
                g_bc, b_bc, orow, yo = kw["g_bc"], kw["b_bc"], kw["orow"], kw["yo"]
                T.op("dve", tt(pre[:], pre[:], g_bc[:], ALU.mult), reads=pre_reads + ["g3"], writes=pre_reads)
                T.op("dve", tt(yo[:], pre[:], b_bc[:], ALU.add), reads=pre_reads + ["b3"], writes=[kw["yoname"]])
                T.dma("sp", out[orow:orow + 128, :], yo[:], reads=[kw["yoname"]], writes=["out_rows"])

        def ffn_block(B, wg_d, wu_d, nch, xT_fn, xT_reads, resid, eps_eff, tail_fn, bg=None, bg_per=2):
            N = nch * 128
            wgv = wg_d.rearrange("(kc p) n -> p kc n", p=128); wuv = wu_d.rearrange("(kc p) n -> p kc n", p=128)
            for f0 in range(0, NFF, 2):
                slot = rr("wring" + str(B["nring"]), B["nring"])
                T.dma("pool", B["wg"][slot][:], wgv[:, :, f0 * 128:(f0 + 2) * 128], writes=[f"wg{slot}"])
                T.dma("pool", B["wu"][slot][:], wuv[:, :, f0 * 128:(f0 + 2) * 128], writes=[f"wu{slot}"])
                for f in range(2):
                    ffc = f0 + f
                    pi = rr("gu", 2)
                    pg, pu = (PS[0], PS[1]) if pi == 0 else (PS[4], PS[5])
                    pgn, pun = ("PS0", "PS1") if pi == 0 else ("PS4", "PS5")
                    T.multi("pe", [mm(pg[:, :N], B["wg"][slot][:, kc, f * 128:(f + 1) * 128], xT_fn(kc), kc == 0, kc == 7) for kc in range(8)],
                            reads=[f"wg{slot}"] + xT_reads, writes=[pgn])
                    T.multi("pe", [mm(pu[:, :N], B["wu"][slot][:, kc, f * 128:(f + 1) * 128], xT_fn(kc), kc == 0, kc == 7) for kc in range(8)],
                            reads=[f"wu{slot}"] + xT_reads, writes=[pun])
                    si = rr("sg", 2)
                    T.op("act", act(B["sg"][si][:, :N], pg[:, :N], AF.Silu), reads=[pgn], writes=[f"sg{si}"])
                    T.op("dve", stt(B["actT"][:, ffc, :N], B["sg"][si][:, :N], 0.5 / ALPHA, pu[:, :N], ALU.mult, ALU.mult),
                         reads=[f"sg{si}", pun], writes=[f"actT{ffc}"])
                    if bg is not None:
                        for _ in range(bg_per):
                            next(bg, None)
            if bg is not None:
                for _ in bg:
                    pass
            act_reads = [f"actT{f}" for f in range(NFF)]
            for c in range(nch):
                pre = B["pre"][c % 2]; pname = f"pre{c % 2}"
                if resid[0] == "dram":
                    xr, xrn = resid[1](c)
                for half in range(2):
                    pd = PS[6 + half]; pdn = f"PS{6 + half}"
                    fns = [mm(pd[:, :], B["actT"][:, f, c * 128:(c + 1) * 128], B["wd"][:, f, half * 512:(half + 1) * 512], f == 0, (f == NFF - 1) and resid[0] != "ident")
                           for f in range(NFF)]
                    rds = act_reads + ["wd"]
                    if resid[0] == "ident":
                        src_fn, src_reads = resid[1], resid[2]
                        fns += [mm(pd[:, j * 128:(j + 1) * 128], src_fn(half * 4 + j, c), idb[:], False, j == 3) for j in range(4)]
                        rds = rds + src_reads + ["idb"]
                    T.multi("pe", fns, reads=rds, writes=[pdn])
                if c >= 1:
                    tail_fn(c - 1, B["pre"][(c - 1) % 2], f"pre{(c - 1) % 2}", "stats")
                for half in range(2):
                    pd = PS[6 + half]; pdn = f"PS{6 + half}"
                    hn = pname + ("a" if half == 0 else "b")
                    if resid[0] == "dram":
                        T.op("dve", tt(pre[:, half * 512:(half + 1) * 512], pd[:, :], xr[:, half * 512:(half + 1) * 512], ALU.add),
                             reads=[pdn, xrn, pname + "a", pname + "b"], writes=[hn])
                    else:
                        T.op("act", act(pre[:, half * 512:(half + 1) * 512], pd[:, :], AF.Copy), reads=[pdn, pname + "a", pname + "b"], writes=[hn])
                if c >= 1:
                    tail_fn(c - 1, B["pre"][(c - 1) % 2], f"pre{(c - 1) % 2}", "rest")
            tail_fn(nch - 1, B["pre"][(nch - 1) % 2], f"pre{(nch - 1) % 2}", "all")

        def ssd_alloc(ph, tag, nfeat):
            S = {}
            S["raw"] = [sbt(ph, f"{tag}raw{i}", [128, 520]) for i in range(2)]
            S["acc"] = [sbt(ph, f"{tag}acc{i}", [128, 512]) for i in range(2)]
            S["feat"] = sbt(ph, f"{tag}feat", [128, nfeat, 512], BF16)
            S["dtx"] = sbt(ph, f"{tag}dtx", [128, 4, 32]); S["dte"] = sbt(ph, f"{tag}dte", [128, 4, 32]); S["dt"] = sbt(ph, f"{tag}dt", [128, 4, 32])
            S["dtv"] = sbt(ph, f"{tag}dtv", [128, 4, 32]); S["dA"] = sbt(ph, f"{tag}dA", [128, 4, 32])
            S["ea"] = sbt(ph, f"{tag}ea", [128, 4, 3, 32]); S["coef"] = sbt(ph, f"{tag}coef", [128, 4, 32])
            S["xdd"] = [sbt(ph, f"{tag}xdd{i}", [128, 512], BF16) for i in range(2)]
            S["btok"] = [sbt(ph, f"{tag}btok{i}", [128, 128], BF16) for i in range(2)]
            S["htmp"] = sbt(ph, f"{tag}htmp", [128, 512])
            return S

        def ssd_front(S, hT_fn, hT_reads, N, ccs, hist, hname, only_hist=False, praw=(0, 1)):
            for (fi, chidx, lhs_fn, wres) in ccs:
                pi = praw[rr("praw" + str(len(praw)), len(praw))]; pr = PS[pi]; prn = f"PS{pi}"
                T.multi("pe", [mm(pr[:, :N], lhs_fn(kc), hT_fn(kc), kc == 0, kc == 7) for kc in range(8)], reads=[wres] + hT_reads, writes=[prn])
                hres = f"{hname}{chidx}"
                if only_hist:
                    T.op("dve", ts(hist[:, chidx, :], pr[:, N - 3:N], hp_bc[:, 0:1], None, ALU.mult), reads=[prn, "hp"], writes=[hres])
                    continue
                ri = rr("raw", 2); raw = S["raw"][ri]
                T.op("act", act(raw[:, 3:3 + N], pr[:, :N], AF.Copy), reads=[prn], writes=[f"raw{ri}b"])
                T.op("dve", cp(raw[:, 0:3], hist[:, chidx, :]), reads=[hres], writes=[f"raw{ri}a"])
                T.op("dve", cp(hist[:, chidx, :], raw[:, N:N + 3]), reads=[f"raw{ri}b"], writes=[hres])
                ai = rr("acc", 2); acc = S["acc"][ai]; an = f"acc{ai}"
                T.op("dve", ts(acc[:, :N], raw[:, 3:3 + N], cw[:, chidx, 3:4], cb[:, chidx:chidx + 1], ALU.mult, ALU.add),
                     reads=[f"raw{ri}b", "cw", "cb"], writes=[an])
                for j in (2, 1, 0):
                    T.op("dve", stt(acc[:, :N], raw[:, j:j + N], cw[:, chidx, j:j + 1], acc[:, :N], ALU.mult, ALU.add),
                         reads=[f"raw{ri}a", f"raw{ri}b", an, "cw"], writes=[an])
                T.op("act", act(S["feat"][:, fi, :N], acc[:, :N], AF.Silu), reads=[an], writes=[f"feat{fi}"])

        def ssd_decay(S, hTc_fn, hT_reads, nch, hs0, nh, vidx0):
            psd = PS[2]
            for c in range(nch):
                T.multi("pe", [mm(psd[:, c * 32:c * 32 + nh], hTc_fn(kc, c), Wdt[:, kc, hs0:hs0 + nh], kc == 0, kc == 7) for kc in range(8)],
                        reads=hT_reads + ["Wdt"], writes=["PS2"])
            pv = psd[:, 0:nch * 32].rearrange("p (c h) -> p c h", h=32)[:, :, 0:nh]
            T.op("dve", tt(S["dtx"][:, :nch, :nh], pv, dtb_bc[:, hs0:hs0 + nh].unsqueeze(1).broadcast_to([128, nch, nh]), ALU.add),
                 reads=["PS2", "dtb"], writes=["dtx"])
            T.op("act", act(S["dte"][:, :nch, :nh], S["dtx"][:, :nch, :nh], AF.Exp), reads=["dtx"], writes=["dte"])
            T.op("act", act(S["dt"][:, :nch, :nh], S["dte"][:, :nch, :nh], AF.Ln, bias=1.0), reads=["dte"], writes=["dt"])
            if vidx0 is not None:
                for c in range(nch):
                    T.op("dve", ts(S["dtv"][:, c, :nh], S["dt"][:, c, :nh], valid_bc[:, vidx0 + c:vidx0 + c + 1], None, ALU.mult),
                         reads=["dt", "valid"], writes=["dtv"])
                dtv = S["dtv"]; dtvn = "dtv"
            else:
                dtv = S["dt"]; dtvn = "dt"
            T.op("dve", tt(S["dA"][:, :nch, :nh], dtv[:, :nch, :nh], A_bc[:, hs0:hs0 + nh].unsqueeze(1).broadcast_to([128, nch, nh]), ALU.mult),
                 reads=[dtvn, "A"], writes=["dA"])
            return dtv, dtvn

        def ssd_decay2(S, nch, nh, dtv, dtvn, acp=(3, 0)):
            p3 = PS[acp[0]]; p3n = f"PS{acp[0]}"; a0 = acp[1]
            for c in range(nch):
                for k, (m_, mn) in enumerate(((trile, "trile"), (trigt, "trigt"), (ones, "ones"))):
                    o0 = a0 + (c * 3 + k) * 32
                    T.op("pe", mm(p3[:, o0:o0 + nh], m_[:], S["dA"][:, c, :nh]), reads=["dA", mn], writes=[p3n])
            p3v = p3[:, a0:a0 + nch * 96].rearrange("p (c k h) -> p c k h", k=3, h=32)
            for c in range(nch):
                T.op("act", act(S["ea"][:, c, :, :nh], p3v[:, c, :, 0:nh], AF.Exp), reads=[p3n], writes=["ea"])
            T.op("dve", tt(S["coef"][:, :nch, :nh], dtv[:, :nch, :nh], S["ea"][:, :nch, 1, :nh], ALU.mult), reads=[dtvn, "ea"], writes=["coef"])

        def ssd_state_chunk(S, g, c, ho, nx, full=None, tp_bank=None, s_bank=None):
            for _ in ssd_state_chunk_gen(S, g, c, ho, nx, full, tp_bank, s_bank):
                pass

        def ssd_state_chunk_gen(S, g, c, ho, nx, full=None, tp_bank=None, s_bank=None, hmul_eng="dve"):
            tpi = tp_bank if tp_bank is not None else (4 if full is not None else 4 + rr("tp", 2))
            tpb = PSb[tpi]; tpn = f"PS{tpi}"
            T.multi("pe", [trp(tpb[:, j * 128:(j + 1) * 128], S["feat"][:, j, c * 128:(c + 1) * 128], idb[:]) for j in range(nx)],
                    reads=[f"feat{j}" for j in range(nx)] + ["idb"], writes=[tpn])
            xi = rr("xdd", 2); xdd = S["xdd"][xi]; btok = S["btok"][xi]
            x3 = tpb[:, 0:512].rearrange("p (h d) -> p h d", d=64)
            T.op("dve", tt(xdd[:].rearrange("p (h d) -> p h d", d=64), x3, S["coef"][:, c, ho:ho + 8].unsqueeze(2).broadcast_to([128, 8, 64]), ALU.mult),
                 reads=[tpn, "coef"], writes=[f"xdd{xi}"])
            T.op("dve", cp(btok[:], tpb[:, 512:640]), reads=[tpn], writes=[f"btok{xi}"])
            if full is not None:
                full(tpb, tpn, x3)
            yield
            si = s_bank if s_bank is not None else (5 if full is not None else 6 + rr("Sps", 2))
            psS = PS[si]; psn = f"PS{si}"
            T.op("pe", mm(psS[:, :], btok[:], xdd[:]), reads=[f"btok{xi}", f"xdd{xi}"], writes=[psn])
            h3 = H[:, g, :].rearrange("p (h d) -> p h d", d=64)
            T.op(hmul_eng, tt(S["htmp"][:].rearrange("p (h d) -> p h d", d=64), h3, S["ea"][:, c, 2, ho:ho + 8].unsqueeze(2).broadcast_to([128, 8, 64]), ALU.mult),
                 reads=[f"H{g}", "ea"], writes=["htmp"])
            T.op("dve", tt(H[:, g, :], S["htmp"][:], psS[:, :], ALU.add), reads=["htmp", psn], writes=[f"H{g}"])

        eps1 = LN_EPS / (ALPHA * ALPHA)
        with ExitStack() as ph:
            B = ffn_alloc(ph, "p", 2)
            S = ssd_alloc(ph, "p", 5)
            xin = [sbt(ph, f"pxin{i}", [128, 1024]) for i in range(2)]
            xres = xin
            xT = sbt(ph, "pxT", [128, 8, 512], BF16)
            hTp = [sbt(ph, f"phT{i}", [128, 8, 512], BF16) for i in range(2)]
            Wxb = sbt(ph, "pWxb", [128, 8, 4, 640], BF16)
            histP = sbt(ph, "phist", [128, 24, 3])
            load_wd(B, f1d)
            for g in range(4):
                T.dma("pool", Wxb[:, :, g, 0:512], win_v[:, :, OFF_XBC + g * 512:OFF_XBC + (g + 1) * 512], writes=[f"Wxb{g}"])
                T.dma("pool", Wxb[:, :, g, 512:640], win_v[:, :, OFF_XBC + 2048 + g * 128:OFF_XBC + 2048 + (g + 1) * 128], writes=[f"Wxb{g}"])
            T.op("dve", lambda e: e.memset(histP[:], 0.0), writes=[f"hP{i}" for i in range(24)])

            def x_load_T(src_rows_fn, nch):
                for c in range(nch):
                    xi = rr("xin", 2)
                    T.dma("sp", xin[xi][:], src_rows_fn(c), writes=[f"xin{xi}"])
                    for half in range(2):
                        pb = PS[2 + half]
                        T.multi("pe", [trp(pb[:, j * 128:(j + 1) * 128], xin[xi][:, (half * 4 + j) * 128:(half * 4 + j + 1) * 128], idf[:]) for j in range(4)],
                                reads=[f"xin{xi}", "idf"], writes=[f"PS{2 + half}"])
                        T.op("act", act(xT[:, half * 4:(half + 1) * 4, c * 128:(c + 1) * 128], pb[:, :].rearrange("p (j t) -> p j t", t=128), AF.Copy),
                             reads=[f"PS{2 + half}"], writes=[f"xT{c}"])

            def make_resid(src_rows_fn):
                def f(c):
                    xi = rr("xin", 2)
                    T.dma("sp", xres[xi][:], src_rows_fn(c), writes=[f"xin{xi}"])
                    return xres[xi], f"xin{xi}"
                return f

            n_pre_blocks = pre_ch // 4

            def state_units(bi, hTb, hnames):
                dtv, dtvn = ssd_decay(S, lambda kc, c: hTb[:, kc, c * 128:(c + 1) * 128], hnames, 4, 0, 32, bi * 4)
                yield
                prev_gen = [None]
                for g in range(4):
                    ccs = [(j, g * 4 + j, (lambda kc, g=g, j=j: Wxb[:, kc, g, j * 128:(j + 1) * 128]), f"Wxb{g}") for j in range(4)]
                    ccs.append((4, 16 + g, (lambda kc, g=g: Wxb[:, kc, g, 512:640]), f"Wxb{g}"))
                    for cc in ccs:
                        ssd_front(S, lambda kc: hTb[:, kc, 0:512], hnames, 512, [cc], histP, "hP", praw=(6,))
                        yield
                    if g == 0:
                        ssd_decay2(S, 4, 32, dtv, dtvn, acp=(2, 128))
                        yield
                    for c in range(4):
                        gen = ssd_state_chunk_gen(S, g, c, g * 8, 5, None, 3, 7, "dve")
                        next(gen)
                        if prev_gen[0] is not None:
                            for _ in prev_gen[0]:
                                pass
                        prev_gen[0] = gen
                        yield
                for _ in prev_gen[0]:
                    pass
                yield

            pending = None
            for pb_i in range(n_pre_blocks):
                rows = lambda c, b0=pb_i * 4: x_prefix[(b0 + c) * 128:(b0 + c + 1) * 128, :]
                x_load_T(rows, 4)
                hTb = hTp[pb_i % 2]; hnames = [f"hTp{pb_i % 2}_{c}" for c in range(4)]

                def tail(c, pre, pname, part, hTb=hTb, slot=pb_i % 2):
                    ln_tail(B, eps1, "hT", pre, pname, part, dst=hTb, dcol=c * 128, gcol=g1c, bcol=b1c, gname=["g1c", "b1c"], dres=f"hTp{slot}_{c}")

                ffn_block(B, f1g, f1u, 4, lambda kc: xT[:, kc, 0:512], [f"xT{c}" for c in range(4)], ("dram", make_resid(rows)), eps1, tail, bg=pending, bg_per=2)
                pending = state_units(pb_i, hTb, hnames)
            if pending is not None:
                for _ in pending:
                    pass
            T.barrier()
        if upto in ("pre_ffn", "pre", "pre_x", "pre_gu", "pre_dn"):
            if upto == "pre":
                dump("H", H[:].rearrange("p g f -> p (g f)"), [128, 2048], ["H0", "H1", "H2", "H3"])
            T.final_wait("sp")
            return nc, dbg_outs
        if "H" in dbg:
            dump("H", H[:].rearrange("p g f -> p (g f)"), [128, 2048], ["H0", "H1", "H2", "H3"])

        main = st
        h1T = sbt(main, "h1T", [128, 8, (own_ch + 1) * 128], BF16)
        for g in range(4):
            T.op("act", act(Hbf[:, g, :], H[:, g, :], AF.Copy), reads=[f"H{g}"], writes=[f"Hbf{g}"])

        with ExitStack() as ph:
            B = ffn_alloc(ph, "a")
            xin = [sbt(ph, f"axin{i}", [128, 1024]) for i in range(2)]
            xres = xin
            xT = sbt(ph, "axT", [128, 8, 512], BF16)
            load_wd(B, f1d)

            def own_rows(ci):
                return x_halo[:, :] if ci == 0 else x_own[(ci - 1) * 128:ci * 128, :]

            def x_load_T2(c0, nch):
                for c in range(nch):
                    xi = rr("xin", 2)
                    T.dma("sp", xin[xi][:], own_rows(c0 + c), writes=[f"xin{xi}"])
                    for half in range(2):
                        pb = PS[2 + half]
                        T.multi("pe", [trp(pb[:, j * 128:(j + 1) * 128], xin[xi][:, (half * 4 + j) * 128:(half * 4 + j + 1) * 128], idf[:]) for j in range(4)],
                                reads=[f"xin{xi}", "idf"], writes=[f"PS{2 + half}"])
                        T.op("act", act(xT[:, half * 4:(half + 1) * 4, c * 128:(c + 1) * 128], pb[:, :].rearrange("p (j t) -> p j t", t=128), AF.Copy),
                             reads=[f"PS{2 + half}"], writes=[f"xT{c}"])

            for (c0, nch) in _split_blocks(own_ch + 1):
                x_load_T2(c0, nch)

                def resid(c, c0=c0):
                    xi = rr("xin", 2)
                    T.dma("sp", xres[xi][:], own_rows(c0 + c), writes=[f"xin{xi}"])
                    return xres[xi], f"xin{xi}"

                def tail(c, pre, pname, part, c0=c0):
                    ln_tail(B, eps1, "hT", pre, pname, part, dst=h1T, dcol=(c0 + c) * 128, gcol=g1c, bcol=b1c, gname=["g1c", "b1c"], dres=f"h1T{c0 + c}")

                ffn_block(B, f1g, f1u, nch, lambda kc, nch=nch: xT[:, kc, 0:nch * 128], [f"xT{c}" for c in range(nch)], ("dram", resid), eps1, tail)
            T.barrier()
        if "h1T" in dbg:
            dump("h1T", h1T[:].rearrange("p k t -> p (k t)"), [128, 8 * (own_ch + 1) * 128], [f"h1T{i}" for i in range(own_ch + 1)])
        if upto == "ffn1":
            T.final_wait("sp")
            return nc, dbg_outs

        p2 = ExitStack()
        accA = sbt(p2, "accA", [128, own_ch, 1024], BF16)
        with ExitStack() as ph:
            Wq = sbt(ph, "Wq", [128, 8, 1024], BF16); Wk2 = sbt(ph, "Wk2", [128, 8, 4, 128], BF16); Wv = sbt(ph, "Wv", [128, 8, 256], BF16)
            Wga = sbt(ph, "Wga", [128, 8, 1024], BF16); Wab = sbt(ph, "Wab", [128, 8, 1024], BF16)
            kT2 = [sbt(ph, f"kT2_{i}", [128, 4, 128], BF16) for i in range(2)]
            Va = [sbt(ph, f"Va{i}", [128, 4, 65], BF16) for i in range(2)]
            qT = sbt(ph, "qT", [128, 8, 128], BF16)
            PT = [sbt(ph, f"PT{i}", [128, 4, 128], BF16) for i in range(4)]
            den = sbt(ph, "den", [128, 4]); rden = sbt(ph, "rden", [128, 4])
            ya = sbt(ph, "ya", [128, 1024], BF16); yaT = sbt(ph, "yaT", [128, 8, 128], BF16); ga = sbt(ph, "ga", [128, 1024])
            T.dma("pool", Wq[:], win_v[:, :, OFF_Q:OFF_Q + 1024], writes=["Wq"])
            for g in range(4):
                T.dma("pool", Wk2[:, :, g, 0:64], win_v[:, :, OFF_K + g * 64:OFF_K + (g + 1) * 64], writes=["Wk2"])
                T.dma("pool", Wk2[:, :, g, 64:128], win_v[:, :, OFF_K + g * 64:OFF_K + (g + 1) * 64], writes=["Wk2"])
            T.dma("pool", Wv[:], win_v[:, :, OFF_V:OFF_V + 256], writes=["Wv"])
            T.dma("pool", Wga[:], win_v[:, :, OFF_GA:OFF_GA + 1024], writes=["Wga"])
            T.dma("pool", Wab[:], w_ab.rearrange("(kc p) n -> p kc n", p=128), writes=["Wab"])
            for i in range(2):
                T.op("dve", lambda e, i=i: e.memset(Va[i][:, :, 64:65], 1.0), writes=[f"Va{i}one"])
            for ci in range(own_ch + 1):
                hc = slice(ci * 128, (ci + 1) * 128); hres = [f"h1T{ci}"]
                s_ = ci % 2
                pk = PS[0]
                for g in range(4):
                    T.multi("pe", [mm(pk[:, g * 128:(g + 1) * 128], Wk2[:, kc, g, :], h1T[:, kc, hc], kc == 0, kc == 7) for kc in range(8)],
                            reads=hres + ["Wk2"], writes=["PS0"])
                T.op("act", act(kT2[s_][:].rearrange("p g t -> p (g t)"), pk[:, :], AF.Copy), reads=["PS0"], writes=[f"kT2_{s_}"])
                pv = PS[1]
                T.multi("pe", [mm(pv[:, 0:256], h1T[:, kc, hc], Wv[:, kc, :], kc == 0, kc == 7) for kc in range(8)], reads=hres + ["Wv"], writes=["PS1"])
                T.op("act", act(Va[s_][:, :, 0:64], pv[:, 0:256].rearrange("p (g d) -> p g d", d=64), AF.Copy), reads=["PS1"], writes=[f"Va{s_}"])
                import os as _os
                _al = int(_os.environ.get("ALVL", "9"))
                if ci == 0 or _al < 2:
                    continue
                c = ci - 1
                for hb in range(2):
                    T.multi("pe", [mm(PS[2 + hb][:, j * 128:(j + 1) * 128], Wq[:, kc, (hb * 4 + j) * 128:(hb * 4 + j + 1) * 128], h1T[:, kc, hc], kc == 0, kc == 7)
                                   for j in range(4) for kc in range(8)], reads=hres + ["Wq"], writes=[f"PS{2 + hb}"])
                    T.op("act", act(qT[:, hb * 4:(hb + 1) * 4, :], PS[2 + hb][:, :].rearrange("p (j t) -> p j t", t=128), AF.Copy),
                         reads=[f"PS{2 + hb}"], writes=[f"qT{hb}"])
                for hb in range(2):
                    T.multi("pe", [mm(PS[hb][:, :], h1T[:, kc, hc], Wga[:, kc, hb * 512:(hb + 1) * 512], kc == 0, kc == 7) for kc in range(8)],
                            reads=hres + ["Wga"], writes=[f"PS{hb}"])
                    T.op("act", act(ga[:, hb * 512:(hb + 1) * 512], PS[hb][:, :], AF.Sigmoid), reads=[f"PS{hb}"], writes=[f"ga{hb}"])
                for g in range(4 if _al >= 3 else 0):
                    ptn = []
                    for half in range(2):
                        pS = PS[4 + half]
                        T.multi("pe", [mm(pS[:, kbi * 256:(kbi + 1) * 256], kT2[ks][half * 64:(half + 1) * 64, g, :],
                                          qT[half * 64:(half + 1) * 64, 2 * g:2 * g + 2, :], True, True) for kbi, ks in enumerate((1 - s_, s_))],
                                reads=["kT2_0", "kT2_1", "qT0", "qT1"], writes=[f"PS{4 + half}"])
                        pi = 2 * (g % 2) + half
                        pt = PT[pi]; ptn.append(f"PT{pi}")
                        T.op("act", act(pt[:].rearrange("p h t -> p (h t)"), pS[:, :], AF.Exp, scale=float(HEAD_SCALE)), reads=[f"PS{4 + half}"], writes=[f"PT{pi}"])
                        mpv, mpn = (mprev, "mprev") if c == 0 else (mprevf, "mprevf")
                        T.op("dve", tt(pt[:, 0:2, :], pt[:, 0:2, :], mpv[:].unsqueeze(1).broadcast_to([128, 2, 128]), ALU.mult),
                             reads=[f"PT{pi}", mpn], writes=[f"PT{pi}"])
                        T.op("dve", tt(pt[:, 2:4, :], pt[:, 2:4, :], mcur[:].unsqueeze(1).broadcast_to([128, 2, 128]), ALU.mult),
                             reads=[f"PT{pi}", "mcur"], writes=[f"PT{pi}"])
                    if _al < 4:
                        continue
                    po = PS[6]
                    fns = []
                    for hl in range(4):
                        half, jj = divmod(hl, 2)
                        for kbi, ks in enumerate((1 - s_, s_)):
                            fns.append(mm(po[:, hl * 65:(hl + 1) * 65], PT[2 * (g % 2) + half][:, kbi * 2 + jj, :], Va[ks][:, g, :], kbi == 0, kbi == 1))
                    T.multi("pe", fns, reads=ptn + ["Va0", "Va1", "Va0one", "Va1one"], writes=["PS6"])
                    if _al < 5:
                        continue
                    pov = po[:, 0:260].rearrange("p (h d) -> p h d", d=65)
                    T.op("dve", tt(den[:].rearrange("p (half jj) -> p half jj", jj=2), pov[:, :, 64].rearrange("p (half jj) -> p half jj", jj=2),
                                   esink_bc[:, 4 * g:4 * g + 4].rearrange("p (jj half) -> p half jj", half=2), ALU.add),
                         reads=["PS6", "esink"], writes=["den"])
                    T.op("dve", lambda e: e.reciprocal(out=rden[:], in_=den[:]), reads=["den"], writes=["rden"])
                    T.op("dve", tt(ya[:, g * 256:(g + 1) * 256].rearrange("p (jj half d) -> p half jj d", jj=2, half=2, d=64),
                                   pov[:, :, 0:64].rearrange("p (half jj) d -> p half jj d", jj=2),
                                   rden[:].rearrange("p (half jj) -> p half jj", jj=2).unsqueeze(3).broadcast_to([128, 2, 2, 64]), ALU.mult),
                         reads=["PS6", "rden"], writes=["ya"])
                if _al < 6:
                    continue
                T.multi("pe", [trp(PSb[7][:, j * 128:(j + 1) * 128], ya[:, j * 128:(j + 1) * 128], idb[:]) for j in range(8)], reads=["ya", "idb"], writes=["PS7"])
                T.op("dve", cp(yaT[:].rearrange("p k t -> p (k t)"), PSb[7][:, 0:1024]), reads=["PS7"], writes=["yaT"])
                for hb in range(2):
                    T.multi("pe", [mm(PS[2 + hb][:, :], yaT[:, kc, :], Wab[:, kc, hb * 512:(hb + 1) * 512], kc == 0, kc == 7) for kc in range(8)],
                            reads=["yaT", "Wab"], writes=[f"PS{2 + hb}"])
                    T.op("dve", stt(accA[:, c, hb * 512:(hb + 1) * 512], PS[2 + hb][:, :], 1.0 / ALPHA, ga[:, hb * 512:(hb + 1) * 512], ALU.mult, ALU.mult),
                         reads=[f"PS{2 + hb}", f"ga{hb}"], writes=[f"accA{c}"])
            T.barrier()
        if upto == "A":
            T.final_wait("sp")
            return nc, dbg_outs
        if "accA" in dbg:
            dump("accA", accA[:].rearrange("p c d -> p (c d)"), [128, own_ch * 1024], [])

        ys = sbt(p2, "ysacc", [128, own_ch, 1024], BF16)
        with ExitStack() as ph:
            S = ssd_alloc(ph, "c", 6)
            Wgrp = sbt(ph, "Wgrp", [128, 8, 1280], BF16)
            Wssm = sbt(ph, "Wssm", [128, 4, 1024], BF16)
            histO = sbt(ph, "histO", [128, 24, 3])
            zs = [sbt(ph, f"zs{i}", [128, 512]) for i in range(2)]; xdt = sbt(ph, "xdt", [128, 512], BF16); xst = sbt(ph, "xst", [128, 512], BF16)
            cbm = sbt(ph, "cbm", [128, 128], BF16); Rr = [sbt(ph, f"Rr{i}", [128, 8, 128]) for i in range(2)]; Lm = sbt(ph, "Lm", [128, 8, 128], BF16); Mm = sbt(ph, "Mm", [128, 8, 128], BF16)
            y1 = [sbt(ph, f"y1_{i}", [128, 512]) for i in range(2)]; y2 = sbt(ph, "y2", [128, 512]); ssq = sbt(ph, "ssq", [128, 8])
            un = sbt(ph, "un", [128, 512], BF16); unT = sbt(ph, "unT", [128, 4, 128], BF16)
            wsb_v = w_sb.rearrange("(kc p) n -> p kc n", p=128)
            v3 = lambda t_: t_[:].rearrange("p (h d) -> p h d", d=64)
            for g in range(4):
                T.dma("pool", Wgrp[:, :, 0:512], win_v[:, :, OFF_XBC + g * 512:OFF_XBC + (g + 1) * 512], writes=["Wgrp"])
                T.dma("pool", Wgrp[:, :, 512:640], win_v[:, :, OFF_XBC + 2048 + g * 128:OFF_XBC + 2048 + (g + 1) * 128], writes=["Wgrp"])
                T.dma("pool", Wgrp[:, :, 640:768], win_v[:, :, OFF_XBC + 2560 + g * 128:OFF_XBC + 2560 + (g + 1) * 128], writes=["Wgrp"])
                T.dma("pool", Wgrp[:, :, 768:1280], win_v[:, :, OFF_Z + g * 512:OFF_Z + (g + 1) * 512], writes=["Wgrp"])
                T.dma("pool", Wssm[:], wsb_v[:, g * 4:(g + 1) * 4, :], writes=["Wssm0"])
                for kc in range(4):
                    T.op("dve", ts(Wssm[:, kc, :], Wssm[:, kc, :], normw[:, g * 4 + kc:g * 4 + kc + 1], None, ALU.mult), reads=["Wssm0", "normw"], writes=["Wssm"])
                ccs = [(j, g * 4 + j, (lambda kc, j=j: Wgrp[:, kc, j * 128:(j + 1) * 128]), "Wgrp") for j in range(4)]
                ccs.append((4, 16 + g, (lambda kc: Wgrp[:, kc, 512:640]), "Wgrp"))
                ccs.append((5, 20 + g, (lambda kc: Wgrp[:, kc, 640:768]), "Wgrp"))
                ssd_front(S, lambda kc: h1T[:, kc, 0:128], ["h1T0"], 128, ccs, histO, "hO", only_hist=True)
                for (t0c, ntc) in _split_blocks(own_ch):
                    tok0 = (1 + t0c) * 128; N = ntc * 128
                    hT_reads = [f"h1T{1 + t0c + c}" for c in range(ntc)]
                    dtv, dtvn = ssd_decay(S, lambda kc, c: h1T[:, kc, tok0 + c * 128:tok0 + (c + 1) * 128], hT_reads, ntc, g * 8, 8, None)
                    ssd_front(S, lambda kc: h1T[:, kc, tok0:tok0 + N], hT_reads, N, ccs, histO, "hO")
                    ssd_decay2(S, ntc, 8, dtv, dtvn)

                    def mk_rr(c):
                        T.op("pool", tt(Rr[c % 2][:], trile[:].unsqueeze(1).broadcast_to([128, 8, 128]), S["dA"][:, c, 0:8].unsqueeze(2).broadcast_to([128, 8, 128]), ALU.mult),
                             reads=["trile", "dA"], writes=[f"Rr{c % 2}"])

                    mk_rr(0)

                    def back_dve(c):
                        b2 = c % 2
                        T.op("dve", lambda e: e.reciprocal(out=ssq[:, b2 * 4 + 2:b2 * 4 + 3], in_=ssq[:, b2 * 4 + 1:b2 * 4 + 2]), reads=[f"ssq{b2}b"], writes=[f"ssq{b2}c"])
                        T.op("dve", ts(un[:], y1[b2][:], ssq[:, b2 * 4 + 2:b2 * 4 + 3], None, ALU.mult), reads=[f"y1{b2}", f"ssq{b2}c"], writes=["un"])

                    def front(c, g=g, t0c=t0c, tok0=tok0, dtv=dtv, dtvn=dtvn, ntc=ntc):
                        cg = t0c + c; b2 = c % 2
                        hc = slice(tok0 + c * 128, tok0 + (c + 1) * 128); csl = slice(c * 128, (c + 1) * 128)
                        T.op("pe", mm(PS[2][:, 256:384], S["feat"][:, 4, csl], S["feat"][:, 5, csl]), reads=["feat4", "feat5"], writes=["PS2"])
                        T.op("dve", tt(cbm[:], PS[2][:, 256:384], trile[:], ALU.mult), reads=["PS2", "trile"], writes=["cbm"])
                        for hb in range(2):
                            T.op("pe", mm(PS[6 + hb][:, :], trigt[:], Rr[b2][:, hb * 4:(hb + 1) * 4, :]), reads=["trigt", f"Rr{b2}"], writes=[f"PS{6 + hb}"])
                            T.op("act", act(Lm[:, hb * 4:(hb + 1) * 4, :], PS[6 + hb][:, :].rearrange("p (h t) -> p h t", t=128), AF.Exp),
                                 reads=[f"PS{6 + hb}"], writes=[f"Lm{hb}"])
                        T.op("pool", tt(Mm[:], Lm[:], cbm[:].unsqueeze(1).broadcast_to([128, 8, 128]), ALU.mult), reads=["Lm0", "Lm1", "cbm"], writes=["Mm"])
                        if c + 1 < ntc:
                            mk_rr(c + 1)
                        T.multi("pe", [mm(PS[5][:, :], h1T[:, kc, hc], Wgrp[:, kc, 768:1280], kc == 0, kc == 7) for kc in range(8)],
                                reads=[f"h1T{1 + cg}", "Wgrp"], writes=["PS5"])
                        T.op("act", act(zs[b2][:], PS[5][:, :], AF.Silu), reads=["PS5"], writes=[f"zs{b2}"])
                        hin, hinn = (Hbf[:, g, :], f"Hbf{g}") if cg % 2 == 0 else (Hbf2[:, :], "Hbf2")
                        hout, houtn = (Hbf2[:, :], "Hbf2") if cg % 2 == 0 else (Hbf[:, g, :], f"Hbf{g}")

                        def full(tpb, tpn, x3, c=c):
                            T.op("dve", tt(v3(xdt), x3, dtv[:, c, 0:8].unsqueeze(2).broadcast_to([128, 8, 64]), ALU.mult), reads=[tpn, dtvn], writes=["xdt"])
                            T.op("dve", cp(xst[:], tpb[:, 0:512]), reads=[tpn], writes=["xst"])
                            if c >= 1:
                                back_dve(c - 1)

                        for _ in ssd_state_chunk_gen(S, g, c, 0, 5, full, None, None, "pool"):
                            pass
                        T.op("act", act(hout, H[:, g, :], AF.Copy), reads=[f"H{g}"], writes=[houtn])
                        T.op("pe", mm(PS[7][:, :], S["feat"][:, 5, csl], hin), reads=["feat5", hinn], writes=["PS7"])
                        T.multi("pe", [mm(PS[6][:, h * 64:(h + 1) * 64], Mm[:, h, :], xdt[:, h * 64:(h + 1) * 64]) for h in range(8)],
                                reads=["Mm", "xdt"], writes=["PS6"])
                        yy = y1[b2]; yn_ = f"y1{b2}"
                        T.op("dve", tt(v3(yy), PS[7][:, :].rearrange("p (h d) -> p h d", d=64), S["ea"][:, c, 0, 0:8].unsqueeze(2).broadcast_to([128, 8, 64]), ALU.mult),
                             reads=["PS7", "ea"], writes=[yn_])
                        T.op("dve", tt(yy[:], yy[:], PS[6][:, :], ALU.add), reads=[yn_, "PS6"], writes=[yn_])
                        T.op("pool", tt(v3(y2), v3(xst), dsk_bc[:, g * 8:(g + 1) * 8].unsqueeze(2).broadcast_to([128, 8, 64]), ALU.mult), reads=["xst", "dsk"], writes=["y2"])
                        T.op("pool", tt(yy[:], yy[:], y2[:], ALU.add), reads=[yn_, "y2"], writes=[yn_])
                        T.op("pool", tt(yy[:], yy[:], zs[b2][:], ALU.mult), reads=[yn_, f"zs{b2}"], writes=[yn_])
                        T.op("act", act(y2[:], yy[:], AF.Square, accum_out=ssq[:, b2 * 4:b2 * 4 + 1]), reads=[yn_, "y2"], writes=["y2", f"ssq{b2}a"])
                        T.op("act", act(ssq[:, b2 * 4 + 1:b2 * 4 + 2], ssq[:, b2 * 4:b2 * 4 + 1], AF.Sqrt, scale=1.0 / 512.0, bias=float(RMS_EPS)), reads=[f"ssq{b2}a"], writes=[f"ssq{b2}b"])

                    def back(c, g=g, t0c=t0c, ntc=ntc):
                        cg = t0c + c; b2 = c % 2
                        if c == ntc - 1:
                            back_dve(c)
                        T.multi("pe", [trp(PSb[4][:, j * 128:(j + 1) * 128], un[:, j * 128:(j + 1) * 128], idb[:]) for j in range(4)], reads=["un", "idb"], writes=["PS4"])
                        T.op("dve", cp(unT[:].rearrange("p k t -> p (k t)"), PSb[4][:, 0:512]), reads=["PS4"], writes=["unT"])
                        for hb in range(2):
                            T.multi("pe", [mm(PS[hb][:, :], unT[:, kc, :], Wssm[:, kc, hb * 512:(hb + 1) * 512], kc == 0, kc == 3) for kc in range(4)],
                                    reads=["unT", "Wssm"], writes=[f"PS{hb}"])
                            if g == 0:
                                T.op("act", act(ys[:, cg, hb * 512:(hb + 1) * 512], PS[hb][:, :], AF.Copy), reads=[f"PS{hb}"], writes=[f"ys{cg}_{hb}"])
                            else:
                                T.op("dve", tt(ys[:, cg, hb * 512:(hb + 1) * 512], ys[:, cg, hb * 512:(hb + 1) * 512], PS[hb][:, :], ALU.add),
                                     reads=[f"PS{hb}", f"ys{cg}_{hb}"], writes=[f"ys{cg}_{hb}"])

                    for c in range(ntc + 1):
                        if c < ntc:
                            front(c)
                        if c >= 1:
                            back(c - 1)
            T.barrier()
        if upto == "C":
            T.final_wait("sp")
            return nc, dbg_outs
        if "ys" in dbg:
            dump("ys", ys[:].rearrange("p c d -> p (c d)"), [128, own_ch * 1024], [])

        with ExitStack() as ph:
            Wgs = sbt(ph, "Wgs", [128, 8, 1024], BF16); Wo = sbt(ph, "Wo", [128, 8, 1024], BF16)
            gs = [sbt(ph, f"gs{i}", [128, 1024]) for i in range(2)]; tmpm = [sbt(ph, f"tmpm{i}", [128, 1024]) for i in range(2)]
            mrg = [sbt(ph, f"mrg{i}", [128, 1024], BF16) for i in range(2)]; mT = [sbt(ph, f"mT{i}", [128, 8, 128], BF16) for i in range(2)]
            B2 = {"pre": [sbt(ph, f"dpre{i}", [128, 1024]) for i in range(2)], "st6": sbt(ph, "dst6", [128, 2, 6]), "mv": sbt(ph, "dmv", [128, 4])}
            T.dma("pool", Wgs[:], win_v[:, :, OFF_GS:OFF_GS + 1024], writes=["Wgs"])
            T.dma("pool", Wo[:], w_o.rearrange("(kc p) n -> p kc n", p=128), writes=["Wo"])

            def d_s1(c):
                b = c % 2
                hc = slice((c + 1) * 128, (c + 2) * 128); hres = [f"h1T{c + 1}"]
                for hb in range(2):
                    T.multi("pe", [mm(PS[hb][:, :], h1T[:, kc, hc], Wgs[:, kc, hb * 512:(hb + 1) * 512], kc == 0, kc == 7) for kc in range(8)],
                            reads=hres + ["Wgs"], writes=[f"PS{hb}"])
                    T.op("act", act(gs[b][:, hb * 512:(hb + 1) * 512], PS[hb][:, :], AF.Sigmoid), reads=[f"PS{hb}"], writes=[f"gs{b}_{hb}"])
                    T.op("dve", stt(tmpm[b][:, hb * 512:(hb + 1) * 512], ys[:, c, hb * 512:(hb + 1) * 512], 1.0 / ALPHA, gs[b][:, hb * 512:(hb + 1) * 512], ALU.mult, ALU.mult),
                         reads=[f"gs{b}_{hb}"], writes=[f"tmpm{b}_{hb}"])
                    T.op("pool", tt(mrg[b][:, hb * 512:(hb + 1) * 512], tmpm[b][:, hb * 512:(hb + 1) * 512], accA[:, c, hb * 512:(hb + 1) * 512], ALU.add),
                         reads=[f"tmpm{b}_{hb}"], writes=[f"mrg{b}_{hb}"])
                T.multi("pe", [trp(PSb[4][:, j * 128:(j + 1) * 128], mrg[b][:, j * 128:(j + 1) * 128], idb[:]) for j in range(8)],
                        reads=[f"mrg{b}_0", f"mrg{b}_1", "idb"], writes=["PS4"])
                T.op("dve", cp(mT[b][:].rearrange("p k t -> p (k t)"), PSb[4][:, 0:1024]), reads=["PS4"], writes=[f"mT{b}"])

            def d_s2(c):
                b = c % 2
                hc = slice((c + 1) * 128, (c + 2) * 128); hres = [f"h1T{c + 1}"]
                for hb in range(2):
                    fns = [mm(PS[6 + hb][:, :], mT[b][:, kc, :], Wo[:, kc, hb * 512:(hb + 1) * 512], kc == 0, False) for kc in range(8)]
                    fns += [mm(PS[6 + hb][:, j * 128:(j + 1) * 128], h1T[:, hb * 4 + j, hc], idb[:], False, j == 3) for j in range(4)]
                    T.multi("pe", fns, reads=[f"mT{b}", "Wo", "idb"] + hres, writes=[f"PS{6 + hb}"])
                    T.op("act", act(B2["pre"][b][:, hb * 512:(hb + 1) * 512], PS[6 + hb][:, :], AF.Copy),
                         reads=[f"PS{6 + hb}", f"pre{b}a", f"pre{b}b"], writes=[f"pre{b}" + ("a" if hb == 0 else "b")])
                ln_tail(B2, eps1, "hT", B2["pre"][b], f"pre{b}", "stats")

            def d_s3(c):
                b = c % 2
                ln_tail(B2, eps1, "hT", B2["pre"][b], f"pre{b}", "rest", dst=h1T, dcol=c * 128, gcol=g2c, bcol=b2c, gname=["g2c", "b2c"], dres=f"h1T{c}")

            for step in range(own_ch + 2):
                if step < own_ch:
                    d_s1(step)
                if 1 <= step <= own_ch:
                    d_s2(step - 1)
                if step >= 2:
                    d_s3(step - 2)
            T.barrier()
        p2.close()
        if upto == "D":
            T.final_wait("sp")
            return nc, dbg_outs
        if "x2T" in dbg:
            dump("x2T", h1T[:, :, 0:own_ch * 128], [128, 8, own_ch * 128], [])

        with ExitStack() as ph:
            B = ffn_alloc(ph, "f", 3)
            g3 = sbt(ph, "g3", [128, 1024]); b3 = sbt(ph, "b3", [128, 1024]); yo = [sbt(ph, f"yo{i}", [128, 1024]) for i in range(2)]
            load_wd(B, f2d)
            T.dma("sp", g3[:], ln3_gr[0, :].partition_broadcast(128), writes=["g3"])
            T.dma("sp", b3[:], ln3_br[0, :].partition_broadcast(128), writes=["b3"])
            for (c0, nch) in _split_blocks(own_ch):
                x2r = [f"h1T{c0 + c}" for c in range(nch)]

                def tail(c, pre, pname, part, c0=c0):
                    if part == "stats":
                        ln_tail(B, eps1, "out", pre, pname, "stats")
                        return
                    yi = rr("yo", 2)
                    ln_tail(B, eps1, "out", pre, pname, part, g_bc=g3, b_bc=b3, orow=(c0 + c) * 128, yo=yo[yi], yoname=f"yo{yi}")

                ffn_block(B, f2g, f2u, nch, lambda kc, c0=c0, nch=nch: h1T[:, kc, c0 * 128:(c0 + nch) * 128], x2r,
                          ("ident", (lambda k, c, c0=c0: h1T[:, k, (c0 + c) * 128:(c0 + c + 1) * 128]), x2r), eps1, tail)
        T.final_wait("sp")
    return nc, dbg_outs


def _make_in_maps(inp, own_ch, pre_ch):
    x = np.asarray(inp["x"], np.float32)
    batch, seq, _ = x.shape
    own = own_ch * CH
    npos = seq // own
    sq = lambda k: np.ascontiguousarray(np.asarray(inp[k], np.float32)[0])
    col = lambda v: np.ascontiguousarray(v.reshape(-1, 128).T)
    shared = dict(_const_inputs())
    shared.update({
        "ffn1_wg": sq("ffn1_w_gate"), "ffn1_wu": sq("ffn1_w_up"), "ffn1_wd": sq("ffn1_w_down"),
        "ffn2_wg": sq("ffn2_w_gate"), "ffn2_wu": sq("ffn2_w_up"), "ffn2_wd": sq("ffn2_w_down"),
        "w_in": sq("w_in"), "w_ab": sq("w_attn_branch"), "w_sb": sq("w_ssm_branch"), "w_o": sq("w_out"),
        "ln1_gc": col(sq("ln1_g")), "ln1_bc": col(sq("ln1_b")), "ln2_gc": col(sq("ln2_g")), "ln2_bc": col(sq("ln2_b")),
        "ln3_gr": sq("ln3_g").reshape(1, -1), "ln3_br": sq("ln3_b").reshape(1, -1),
        "conv_wc": np.ascontiguousarray(sq("conv_w").reshape(4, 24, 128).transpose(2, 1, 0).reshape(128, 96)),
        "conv_bc": col(sq("conv_b")),
        "dtb_r": sq("dt_bias").reshape(1, -1), "alog_r": sq("a_log").reshape(1, -1), "dsk_r": sq("d_skip").reshape(1, -1),
        "sinks_r": sq("sinks").reshape(1, -1), "normw_c": col(sq("ssm_norm_w")),
    })
    maps = []
    for c in range(batch * npos):
        b, p = divmod(c, npos)
        t0 = p * own
        m = dict(shared)
        m["x_own"] = np.ascontiguousarray(x[b, t0:t0 + own])
        halo = np.zeros((HALO, D), np.float32)
        if p > 0:
            halo[:] = x[b, t0 - HALO:t0]
        m["x_halo"] = halo
        pre = np.zeros((pre_ch * CH, D), np.float32)
        if p > 0:
            pre[:t0] = x[b, :t0]
        m["x_prefix"] = pre
        pv = np.zeros((1, pre_ch), np.float32); pv[0, :t0 // CH] = 1.0
        m["pre_valid"] = pv
        m["has_prev"] = np.full((1, 1), 1.0 if p > 0 else 0.0, np.float32)
        maps.append(m)
    return maps


def kernel(**inputs):
    maps = _make_in_maps(inputs, N_OWN_CH, N_PRE_CH)
    nc = build(N_OWN_CH, N_PRE_CH)
    if isinstance(nc, tuple):
        nc = nc[0]
    res = run_bass_kernel_spmd(nc, maps, core_ids=list(range(NCORES)))
    x = np.asarray(inputs["x"])
    outp = np.empty(x.shape, np.float32)
    npos = SEQ // OWN
    for c in range(NCORES):
        b, p = divmod(c, npos)
        outp[b, p * OWN:(p + 1) * OWN] = res.results[c]["out"]
    return outp
```
